# Optimizing a Trainium2 kernel written in Bass

```python
import math
import jax, jax.numpy as jnp
from jax import lax
import numpy as np

D_MODEL = 1024
BATCH = 32
SEQ = 2048
DEPTH = 4
DEC_BATCH = 16
DEC_SEQ = 64
PAST_LEN = 2048

CHUNK = 64
N_MIXERS = 3
N_RET = (DEPTH + 2) // 3
N_POOL = (DEPTH + 1) // 3
N_ATT = DEPTH // 3
D_FF = 2816
EPS = 1e-6
RET_HEADS = 4
RET_DK = D_MODEL // RET_HEADS
RET_DV = 2 * D_MODEL // RET_HEADS
ROPE_BASE = 10000.0
POOL_WINDOWS = (2, 4, 8, 16)
POOL_GROUPS = 4
POOL_GC = D_MODEL // POOL_GROUPS
POOL_BUF = 15
ATT_HEADS = 16
ATT_HD = D_MODEL // ATT_HEADS
LEFT_CHUNKS = 8
BAND_ROWS = LEFT_CHUNKS * CHUNK
REL_MIN = -(CHUNK - 1)
REL_MAX = 256
REL_SIZE = REL_MAX - REL_MIN + 1
NEG_INF = -1e30

kernel_name = 'hybrid_streaming_encoder_step'

F32 = jnp.float32


def rmsnorm(x, g):
    xf = x.astype(F32)
    y = xf * lax.rsqrt(jnp.mean(xf * xf, axis=-1, keepdims=True) + EPS) * g.astype(F32)
    return y.astype(x.dtype)


def swiglu(h, w_in, w_out):
    gate, up = jnp.split(h @ w_in, 2, axis=-1)
    return (jax.nn.silu(gate) * up) @ w_out


def rotary(x, pos):
    half = x.shape[-1] // 2
    inv = ROPE_BASE ** (-jnp.arange(half, dtype=F32) / half)
    ang = pos.astype(F32)[:, None] * inv[None, :]
    cos = jnp.cos(ang)[None, :, None, :]
    sin = jnp.sin(ang)[None, :, None, :]
    xf = x.astype(F32)
    x1, x2 = xf[..., :half], xf[..., half:]
    return jnp.concatenate([x1 * cos - x2 * sin, x1 * sin + x2 * cos], axis=-1)


def retention(h, s0, pos0, w_in, w_out, gn_g):
    b, t, _ = h.shape
    lc = min(CHUNK, t)
    nc = t // lc
    q, k, v, g = jnp.split(h @ w_in, [D_MODEL, 2 * D_MODEL, 4 * D_MODEL], axis=-1)
    pos = pos0 + jnp.arange(t)
    q = rotary(q.reshape(b, t, RET_HEADS, RET_DK), pos)
    k = rotary(k.reshape(b, t, RET_HEADS, RET_DK), pos) * (RET_DK ** -0.5)
    v = v.reshape(b, t, RET_HEADS, RET_DV).astype(F32)

    def to_chunks(a):
        return a.reshape(b, nc, lc, RET_HEADS, a.shape[-1]).transpose(1, 0, 3, 2, 4)

    gamma = 1.0 - 2.0 ** (-5.0 - jnp.arange(RET_HEADS, dtype=F32))
    idx = jnp.arange(lc, dtype=F32)
    diff = idx[:, None] - idx[None, :]
    decay = jnp.where(diff >= 0, gamma[:, None, None] ** jnp.maximum(diff, 0.0), 0.0)
    q_dec = gamma[:, None] ** (idx + 1.0)
    k_dec = gamma[:, None] ** (lc - 1.0 - idx)
    c_dec = gamma ** float(lc)

    def step(s, qkv):
        qc, kc, vc = qkv
        inner = jnp.einsum('bhid,bhjd->bhij', qc, kc) * decay
        o = jnp.einsum('bhij,bhje->bhie', inner, vc) + jnp.einsum('bhid,bhde->bhie', qc * q_dec[:, :, None], s)
        s = s * c_dec[:, None, None] + jnp.einsum('bhjd,bhje->bhde', kc * k_dec[:, :, None], vc)
        return s, o

    s_fin, o = lax.scan(step, s0.astype(F32), (to_chunks(q), to_chunks(k), to_chunks(v)))
    o = o.transpose(1, 0, 3, 2, 4).reshape(b, t, RET_HEADS, RET_DV)
    mu = jnp.mean(o, axis=-1, keepdims=True)
    var = jnp.mean(jnp.square(o - mu), axis=-1, keepdims=True)
    o = (o - mu) * lax.rsqrt(var + EPS) * gn_g.astype(F32)
    y = (jax.nn.silu(g) * o.reshape(b, t, 2 * D_MODEL).astype(h.dtype)) @ w_out
    return y, s_fin


def pool_mixer(h, buf, pos0, w_grp, b_grp, scale):
    bsz, t, _ = h.shape
    ext = jnp.concatenate([buf.astype(h.dtype), h], axis=1)
    cs = jnp.cumsum(ext.astype(F32), axis=1)
    cs = jnp.concatenate([jnp.zeros((bsz, 1, D_MODEL), F32), cs], axis=1)
    pos = pos0 + jnp.arange(t)
    upper = cs[:, POOL_BUF + 1:POOL_BUF + 1 + t]
    means = []
    for gi, w in enumerate(POOL_WINDOWS):
        ch = slice(gi * POOL_GC, (gi + 1) * POOL_GC)
        wsum = upper[..., ch] - cs[:, POOL_BUF + 1 - w:POOL_BUF + 1 - w + t, ch]
        cnt = jnp.minimum(pos + 1, w).astype(F32)[None, :, None]
        means.append(wsum / cnt)
    pooled = jnp.stack(means, axis=2) - h.astype(F32).reshape(bsz, t, POOL_GROUPS, POOL_GC)
    y = jnp.einsum('btgc,gce->btge', pooled.astype(h.dtype), w_grp).reshape(bsz, t, D_MODEL) + b_grp
    return y * scale, ext[:, -POOL_BUF:]


def att_project(h, w_qkv, q_g, k_g):
    b, t, _ = h.shape
    q, k, v = jnp.split(h @ w_qkv, 3, axis=-1)
    q = rmsnorm(q.reshape(b, t, ATT_HEADS, ATT_HD), q_g)
    k = rmsnorm(k.reshape(b, t, ATT_HEADS, ATT_HD), k_g)
    return q, k, v.reshape(b, t, ATT_HEADS, ATT_HD)


def band_attend(q, k, v, q_pos, k_pos, rel_bias):
    s = jnp.einsum('bqhd,bkhd->bhqk', q, k).astype(F32) * (ATT_HD ** -0.5)
    rel = jnp.clip(q_pos[:, None] - k_pos[None, :], REL_MIN, REL_MAX) - REL_MIN
    s = s + rel_bias[:, rel].astype(F32)[None]
    qc = jnp.floor_divide(q_pos, CHUNK)[:, None]
    kc = jnp.floor_divide(k_pos, CHUNK)[None, :]
    ok = (k_pos[None, :] >= 0) & (kc <= qc) & (kc >= qc - LEFT_CHUNKS)
    s = jnp.where(ok[None, None], s, NEG_INF)
    p = jax.nn.softmax(s, axis=-1)
    return jnp.einsum('bhqk,bkhd->bqhd', p.astype(v.dtype), v)


def chunk_attention_prompt(q, k, v, rel_bias):
    b, t = q.shape[:2]
    nc = t // CHUNK
    pad = ((0, 0), (BAND_ROWS, 0), (0, 0), (0, 0))
    kp = jnp.pad(k, pad)
    vp = jnp.pad(v, pad)
    band = BAND_ROWS + CHUNK

    def one(n):
        start = n * CHUNK
        qn = lax.dynamic_slice_in_dim(q, start, CHUNK, axis=1)
        kn = lax.dynamic_slice_in_dim(kp, start, band, axis=1)
        vn = lax.dynamic_slice_in_dim(vp, start, band, axis=1)
        q_pos = start + jnp.arange(CHUNK)
        k_pos = start - BAND_ROWS + jnp.arange(band)
        return band_attend(qn, kn, vn, q_pos, k_pos, rel_bias)

    o = lax.map(one, jnp.arange(nc))
    return o.transpose(1, 0, 2, 3, 4).reshape(b, t, ATT_HEADS, ATT_HD)


def trunk(x, c, pos0, ret_states, pool_bufs, att_k_cache, att_v_cache, p):
    b, t, _ = x.shape
    new_ret, new_pool, new_k, new_v = [], [], [], []
    c_act = jax.nn.silu(c)
    for i in range(DEPTH):
        mod = c_act @ p['ada_w'][i] + p['ada_b'][i]
        sh1, sc1, g1, shm, scm, gm, sh2, sc2, g2 = [m[:, None, :] for m in jnp.split(mod, 9, axis=-1)]
        h = rmsnorm(x, p['norm_g'][i, 0]) * (1.0 + sc1) + sh1
        x = x + 0.5 * g1 * swiglu(h, p['ffn1_w_in'][i], p['ffn1_w_out'][i])
        h = rmsnorm(x, p['norm_g'][i, 1]) * (1.0 + scm) + shm
        j = i // N_MIXERS
        kind = i % N_MIXERS
        if kind == 0:
            y, s_new = retention(h, ret_states[j], pos0, p['ret_w_in'][j], p['ret_w_out'][j], p['ret_gn_g'][j])
            new_ret.append(s_new.astype(x.dtype))
        elif kind == 1:
            y, buf_new = pool_mixer(h, pool_bufs[j], pos0, p['pool_w'][j], p['pool_b'][j], p['pool_scale'][j])
            new_pool.append(buf_new)
        else:
            q, k, v = att_project(h, p['att_w_qkv'][j], p['att_q_g'][j], p['att_k_g'][j])
            rb = p['att_rel_bias'][j]
            if att_k_cache is None:
                o = chunk_attention_prompt(q, k, v, rb)
                keep = min(BAND_ROWS, t)
                new_k.append(k[:, t - keep:])
                new_v.append(v[:, t - keep:])
            else:
                ck = att_k_cache[j].astype(k.dtype)
                cv = att_v_cache[j].astype(v.dtype)
                wc = ck.shape[1]
                kb = jnp.concatenate([ck, k], axis=1)
                vb = jnp.concatenate([cv, v], axis=1)
                k_pos = jnp.concatenate([pos0 - wc + jnp.arange(wc), pos0 + jnp.arange(t)])
                q_pos = pos0 + jnp.arange(t)
                o = band_attend(q, kb, vb, q_pos, k_pos, rb)
                new_k.append(k)
                new_v.append(v)
            y = o.reshape(b, t, D_MODEL) @ p['att_w_o'][j]
        x = x + gm * y
        h = rmsnorm(x, p['norm_g'][i, 2]) * (1.0 + sc2) + sh2
        x = x + 0.5 * g2 * swiglu(h, p['ffn2_w_in'][i], p['ffn2_w_out'][i])
    return x, jnp.stack(new_ret), jnp.stack(new_pool), jnp.stack(new_k), jnp.stack(new_v)


def setup_inputs(seed: int = 0) -> dict:
    key = jax.random.key(seed)
    ks = jax.random.split(key, 26)

    def nrm(k, shape, scale):
        return jax.random.normal(k, shape, F32) * scale

    wc = min(BAND_ROWS, PAST_LEN)
    d = D_MODEL
    return {
        'x_prompt': nrm(ks[0], (BATCH, SEQ, d), 1.0),
        'x_sample': nrm(ks[1], (DEC_BATCH, DEC_SEQ, d), 1.0),
        'c_prompt': nrm(ks[2], (BATCH, d), 1.0),
        'c_sample': nrm(ks[3], (DEC_BATCH, d), 1.0),
        'state_ret': nrm(ks[4], (N_RET, DEC_BATCH, RET_HEADS, RET_DK, RET_DV), 0.5),
        'state_pool': nrm(ks[5], (N_POOL, DEC_BATCH, POOL_BUF, d), 1.0),
        'cache_att_k': nrm(ks[6], (N_ATT, DEC_BATCH, wc, ATT_HEADS, ATT_HD), 1.0),
        'cache_att_v': nrm(ks[7], (N_ATT, DEC_BATCH, wc, ATT_HEADS, ATT_HD), 1.0),
        'norm_g': 1.0 + nrm(ks[8], (DEPTH, 3, d), 0.02),
        'ada_w': nrm(ks[9], (DEPTH, d, 9 * d), 0.5 * d ** -0.5),
        'ada_b': nrm(ks[10], (DEPTH, 9 * d), 0.02),
        'ffn1_w_in': nrm(ks[11], (DEPTH, d, 2 * D_FF), d ** -0.5),
        'ffn1_w_out': nrm(ks[12], (DEPTH, D_FF, d), D_FF ** -0.5),
        'ffn2_w_in': nrm(ks[13], (DEPTH, d, 2 * D_FF), d ** -0.5),
        'ffn2_w_out': nrm(ks[14], (DEPTH, D_FF, d), D_FF ** -0.5),
        'ret_w_in': nrm(ks[15], (N_RET, d, 6 * d), d ** -0.5),
        'ret_w_out': nrm(ks[16], (N_RET, 2 * d, d), (2 * d) ** -0.5),
        'ret_gn_g': 1.0 + nrm(ks[17], (N_RET, RET_HEADS, RET_DV), 0.02),
        'pool_w': nrm(ks[18], (N_POOL, POOL_GROUPS, POOL_GC, POOL_GC), POOL_GC ** -0.5),
        'pool_b': nrm(ks[19], (N_POOL, d), 0.02),
        'pool_scale': 1.0 + nrm(ks[20], (N_POOL, d), 0.02),
        'att_w_qkv': nrm(ks[21], (N_ATT, d, 3 * d), d ** -0.5),
        'att_w_o': nrm(ks[22], (N_ATT, d, d), d ** -0.5),
        'att_q_g': 1.0 + nrm(ks[23], (N_ATT, ATT_HD), 0.02),
        'att_k_g': 1.0 + nrm(ks[24], (N_ATT, ATT_HD), 0.02),
        'att_rel_bias': nrm(ks[25], (N_ATT, ATT_HEADS, REL_SIZE), 0.5),
    }


def reference(x_prompt, x_sample, c_prompt, c_sample, state_ret, state_pool, cache_att_k, cache_att_v,
              norm_g, ada_w, ada_b, ffn1_w_in, ffn1_w_out, ffn2_w_in, ffn2_w_out,
              ret_w_in, ret_w_out, ret_gn_g, pool_w, pool_b, pool_scale,
              att_w_qkv, att_w_o, att_q_g, att_k_g, att_rel_bias):
    p = {
        'norm_g': norm_g, 'ada_w': ada_w, 'ada_b': ada_b,
        'ffn1_w_in': ffn1_w_in, 'ffn1_w_out': ffn1_w_out,
        'ffn2_w_in': ffn2_w_in, 'ffn2_w_out': ffn2_w_out,
        'ret_w_in': ret_w_in, 'ret_w_out': ret_w_out, 'ret_gn_g': ret_gn_g,
        'pool_w': pool_w, 'pool_b': pool_b, 'pool_scale': pool_scale,
        'att_w_qkv': att_w_qkv, 'att_w_o': att_w_o, 'att_q_g': att_q_g, 'att_k_g': att_k_g,
        'att_rel_bias': att_rel_bias,
    }
    bp = x_prompt.shape[0]
    ret0 = jnp.zeros((N_RET, bp, RET_HEADS, RET_DK, RET_DV), F32)
    pool0 = jnp.zeros((N_POOL, bp, POOL_BUF, D_MODEL), x_prompt.dtype)
    y_prompt, ret_p, pool_p, k_p, v_p = trunk(x_prompt, c_prompt, 0, ret0, pool0, None, None, p)
    y_sample, ret_s, pool_s, k_s, v_s = trunk(x_sample, c_sample, PAST_LEN, state_ret, state_pool,
                                              cache_att_k, cache_att_v, p)
    return (y_prompt, y_sample, ret_p, ret_s, pool_p, pool_s, k_p, v_p, k_s, v_s)
```

```python
import numpy as np
import concourse.bass as bass
import concourse.mybir as mybir
from concourse.bass_utils import run_bass_kernel_spmd
from contextlib import ExitStack

F32 = mybir.dt.float32
BF16 = mybir.dt.bfloat16
AF = mybir.ActivationFunctionType
ALU = mybir.AluOpType

D = 1024
DFF = 2816
NCH = 8
EPS = 1e-6
TILE = 512
NSLOT = 4
SLOT = 4096
RET_H = 4
ATT_H = 16


class Prog:
    def __init__(self):
        self.ops = []
        self.last_w = {}
        self.readers = {}
        self.dma_cnt = {}

    def seed(self, new_keys, old_keys):
        lw = set()
        rd = {}
        for k in old_keys:
            lw |= self.last_w.get(k, set())
            for e, i in self.readers.get(k, {}).items():
                if rd.get(e, -1) < i:
                    rd[e] = i
        for k in new_keys:
            self.last_w[k] = set(lw) | self.last_w.get(k, set())
            r = dict(self.readers.get(k, {}))
            for e, i in rd.items():
                if r.get(e, -1) < i:
                    r[e] = i
            self.readers[k] = r

    def add(self, eng, fn, reads=(), writes=(), dma=None):
        idx = len(self.ops)
        raw = set()
        oth = set()
        for r in reads:
            raw |= self.last_w.get(r, set())
        for w in writes:
            oth |= self.last_w.get(w, set())
            for e, ri in self.readers.get(w, {}).items():
                oth.add(ri)
        waits = []
        for d in raw | oth:
            Pd = self.ops[d]
            if Pd['dma'] is not None:
                waits.append(('dma', Pd['dma'], 16 * self.dma_cnt[Pd['dma']]))
            else:
                if Pd['eng'] == eng and eng == 'pe':
                    continue
                Pd['signal'] = True
                waits.append(('op', d))
        op = dict(eng=eng, fn=fn, dma=dma, signal=False, waits=waits)
        if dma is not None:
            self.dma_cnt[dma] = self.dma_cnt.get(dma, 0) + 1
        self.ops.append(op)
        ek = eng if dma is None else ('dma', dma)
        for r in reads:
            self.readers.setdefault(r, {})[ek] = idx
        for w in writes:
            self.last_w[w] = {idx}
            self.readers[w] = {}
        return idx

    def emit(self, nc, stack):
        engs = ['pe', 'act', 'dve', 'pool', 'sp']
        esem = {e: stack.enter_context(nc.semaphore('s_' + e)) for e in engs}
        dsem = {k: stack.enter_context(nc.semaphore('d_%d' % i)) for i, k in enumerate(self.dma_cnt)}
        cnt = {e: 0 for e in engs}
        for op in self.ops:
            if op['signal'] and op['dma'] is None:
                cnt[op['eng']] += 1
                op['sigval'] = cnt[op['eng']]
        per = {e: [] for e in engs}
        for op in self.ops:
            per[op['eng']].append(op)
        ops = self.ops
        dma_cnt = self.dma_cnt
        block = stack.enter_context(nc.Block())

        def run(eng_name, eng):
            waited = {}
            for op in per[eng_name]:
                need = {}
                for w in op['waits']:
                    if w[0] == 'dma':
                        key = ('d', w[1])
                        val = w[2]
                        sem = dsem[w[1]]
                    else:
                        Pd = ops[w[1]]
                        key = ('e', Pd['eng'])
                        val = Pd['sigval']
                        sem = esem[Pd['eng']]
                    if need.get(key, (None, 0))[1] < val:
                        need[key] = (sem, val)
                for key, (sem, val) in need.items():
                    if waited.get(key, 0) < val:
                        eng.wait_ge(sem, val)
                        waited[key] = val
                ins = op['fn'](eng)
                if op['dma'] is not None:
                    ins.then_inc(dsem[op['dma']], 16)
                elif op['signal']:
                    ins.then_inc(esem[eng_name], 1)
            if eng_name == 'sp':
                for k, c in dma_cnt.items():
                    eng.wait_ge(dsem[k], 16 * c)

        @block.tensor
        def _(e):
            run('pe', e)

        @block.scalar
        def _(e):
            run('act', e)

        @block.vector
        def _(e):
            run('dve', e)

        @block.gpsimd
        def _(e):
            run('pool', e)

        @block.sync
        def _(e):
            run('sp', e)


def weight_plan():
    plan = []

    def ffn(l, w):
        for b in range(11):
            plan.append(('ffn%d_in' % w, l, b, 8, 512))
        for b in range(8):
            plan.append(('ffn%d_out' % w, l, b, 22, 128))

    for l in range(4):
        ffn(l, 1)
        if l % 3 == 0:
            j = l // 3
            for b in range(12):
                plan.append(('ret_in', j, b, 8, 512))
            for b in range(4):
                plan.append(('ret_out', j, b, 16, 256))
        elif l % 3 == 2:
            for b in range(6):
                plan.append(('att_qkv', 0, b, 8, 512))
            for b in range(2):
                plan.append(('att_o', 0, b, 8, 512))
        ffn(l, 2)
    offs = []
    o = 0
    for (_, _, _, kc, bw) in plan:
        offs.append(o)
        o += 128 * kc * bw
    return plan, offs, o


def host_block(inputs, name, l, b, kc, bw):
    if name in ('ffn1_in', 'ffn2_in'):
        W = inputs[name[:4] + '_w_in'][l]
        cols = np.concatenate([np.arange(256 * b, 256 * b + 256), DFF + np.arange(256 * b, 256 * b + 256)])
        Wb = W[:, cols]
    elif name in ('ffn1_out', 'ffn2_out'):
        W = inputs[name[:4] + '_w_out'][l]
        Wb = W[:, 128 * b:128 * b + 128]
    elif name == 'ret_in':
        Wb = inputs['ret_w_in'][l][:, 512 * b:512 * b + 512]
    elif name == 'ret_out':
        Wb = inputs['ret_w_out'][l][:, 256 * b:256 * b + 256]
    elif name == 'att_qkv':
        Wb = inputs['att_w_qkv'][0][:, 512 * b:512 * b + 512]
    elif name == 'att_o':
        Wb = inputs['att_w_o'][0][:, 512 * b:512 * b + 512]
    return np.ascontiguousarray(Wb.reshape(kc, 128, bw).transpose(1, 0, 2)).reshape(-1)


class Cfg:
    def __init__(self, nseq=4, seq=2048, sample=True, dbg=False, limit=10 ** 9):
        self.limit = limit
        self.att_stop = 99
        self.nseq = nseq
        self.seq = seq
        self.sample = sample
        self.dbg = dbg
        self.ns = nseq + 2


def build(cfg):
    nc = bass.Bass("TRN2", target_bir_lowering=False)
    P = Prog()
    NS = cfg.ns
    NSEQ = cfg.nseq
    SEQ = cfg.seq
    NT = SEQ // TILE
    plan, offs, wtot = weight_plan()
    CW = 8192
    wrows = (wtot + CW - 1) // CW
    wrows = ((wrows + 1023) // 1024) * 1024

    def din(name, shape):
        return nc.dram_tensor(name, list(shape), F32, kind="ExternalInput")

    def dout(name, shape):
        return nc.dram_tensor(name, list(shape), F32, kind="ExternalOutput")

    xT_d = din("xT", [NSEQ, D, SEQ]).ap()
    xsT_d = din("xsT", [2, D, 64]).ap()
    cT_d = din("cT", [128, NCH, NS]).ap()
    sret_d = din("sret", [2, 2, RET_H, 256, 512]).ap()
    spool_d = din("spoolT", [2, D, 15]).ap()
    ckT_d = din("ckT", [2, D, 512]).ap()
    cv_d = din("cv", [2, 512, D]).ap()
    NWP = 4
    assert wrows % NWP == 0
    wprows = wrows // NWP
    wall_ds = [din("wall%d" % i, [wprows, CW]).ap() for i in range(NWP)]
    adaw_d = din("ada_w", [4, D, 9 * D]).ap()
    adab_d = din("adabT", [128, 4, 72]).ap()
    normg_d = din("normgT", [128, 4, 3, NCH]).ap()
    poolw_d = din("poolw", [128, 4, 2, 256]).ap()
    poolv_d = din("poolv", [128, 2, NCH]).ap()
    gqk_d = din("gqk", [128, 2]).ap()
    relb_d = din("relb", [16, 320]).ap()
    cf_d = din("cf", [128, 2, SEQ + 64]).ap()
    cm_d = din("cm", [128, 1152]).ap()
    cmb_d = din("cmb", [128, 896]).ap()
    gngT_d = din("gngT", [128, 2, 16]).ap()

    yT_d = dout("yT", [NSEQ, D, SEQ]).ap()
    ysT_d = dout("ysT", [2, D, 64]).ap()
    retp_d = dout("retp", [2, NSEQ, RET_H, 256, 512]).ap()
    rets_d = dout("rets", [2, 2, RET_H, 256, 512]).ap()
    poolp_d = dout("poolpT", [NSEQ, D, 15]).ap()
    pools_d = dout("poolsT", [2, D, 15]).ap()
    kp_d = dout("kpT", [NSEQ, D, 512]).ap()
    vp_d = dout("vp", [NSEQ, 512, D]).ap()
    ks_d = dout("ksT", [2, D, 64]).ap()
    vs_d = dout("vs", [2, 64, D]).ap()

    wsc_t = nc.dram_tensor("wsc", [wrows, CW], BF16, kind="Internal")
    wsc_d = wsc_t.ap()
    rext_t = nc.dram_tensor("rext", [16, 640], F32, kind="Internal")
    carK_d = nc.dram_tensor("carK", [128, NCH, 512], BF16, kind="Internal").ap()
    carV_d = nc.dram_tensor("carV", [128, 4, ATT_H * 72], BF16, kind="Internal").ap()

    st = ExitStack()
    with st:
        def sb(name, shape, dt):
            return st.enter_context(nc.sbuf_tensor(name, list(shape), dt))

        xT = sb("xT_s", [128, NCH, TILE], F32)
        hT = sb("hT_s", [128, NCH, TILE], BF16)
        SCRN = 24576
        scr = sb("scr", [128, SCRN], BF16)
        ws = sb("ws", [128, NSLOT, SLOT], BF16)
        Sst = sb("Sst", [128, 2, RET_H, 2, 512], F32)
        Sbf = sb("Sbf", [128, RET_H, 2, 512], BF16)
        rstd = sb("rstd", [128, TILE], F32)
        sq = sb("sq", [128, 2, TILE], BF16)
        tmpf = sb("tmpf", [128, 4, TILE], F32)
        Btab = sb("Btab", [128, 4, 3, NCH, NS], F32)
        Atab = sb("Atab", [128, 4, 3, NCH, NS], F32)
        Gtab = sb("Gtab", [128, 4, 3, NCH, NS], F32)
        cT = sb("cT_s", [128, NCH, NS], F32)
        cact = sb("cact", [128, NCH, NS], BF16)
        adab = sb("adab", [128, 4, 72], F32)
        normg = sb("normg", [128, 4, 3, NCH], F32)
        poolw = sb("poolw_s", [128, 4, 2, 256], BF16)
        poolv = sb("poolv_s", [128, 2, NCH], F32)
        poolA = sb("poolA", [128, 2, NCH, NS], F32)
        poolcar = sb("poolcar", [128, NCH, 15], F32)
        gqk = sb("gqk_s", [128, 2], F32)
        gngT = sb("gngT_s", [128, 2, 16], F32)
        cm = sb("cm_s", [128, 1152], F32)
        mod = scr[:, 0:2 * 4 * 72 * NS].bitcast(F32).rearrange("p (l n s) -> p l n s", l=4, n=72)
        poolw32 = scr[:, 4096:4096 + 4096].bitcast(F32).rearrange("p (g c e) -> p g c e", g=4, c=2)
        cmb = scr[:, 22528:22528 + 1792].bitcast(F32)
        cs = sb("cs", [128, 2, TILE], F32)
        ident = sb("ident", [128, 128], BF16)
        ones = sb("ones", [128, 128], BF16)
        bdiag = sb("bdiag", [128, 128], BF16)
        cmask = sb("cmask", [128, 4, 128], BF16)
        bias = sb("bias", [128, 7, ATT_H * 64], BF16)
        ktok = sb("ktok", [128, 1024], BF16)
        innT = sb("innT", [128, 4, 128], BF16)
        stat = sb("stat", [128, 4, 8], F32)
        mv = sb("mv", [128, 4, 2], F32)
        rsd = sb("rsd", [128, 4], F32)
        rec = sb("rec", [128, 16], F32)
        osb = sb("osb", [128, 2, 512], F32)
        ps = st.enter_context(nc.psum_tensor("ps", [128, 8, 512], F32))
        psb = ps[:].bitcast(BF16)

        o = 0
        qdec = cm[:, o:o + 512].rearrange("p (h n) -> p h n", h=4); o += 512
        kdec = cm[:, o:o + 512].rearrange("p (h n) -> p h n", h=4); o += 512
        pratio = cm[:, o:o + 128].rearrange("p (c n) -> p c n", c=8); o += 128
        cm_ident = cmb[:, 0:128]
        cm_ones = cmb[:, 128:256]
        cm_bdiag = cmb[:, 256:384]
        cm_cmask = cmb[:, 384:896]

        rotA = [0]
        rotB = [0]

        def bankA():
            b = rotA[0] % 5
            rotA[0] += 1
            return b

        def bankB():
            b = 5 + rotB[0] % 3
            rotB[0] += 1
            return b

        tmpi = [0]

        def tmp32():
            i = tmpi[0] % 4
            tmpi[0] += 1
            return i

        def dma(q, out, in_, reads, writes, key):
            P.add(q, lambda e: e.dma_start(out=out, in_=in_), reads, writes, dma=(key, q))

        def mm(out, lhsT, rhs, start, stop, reads, writes):
            P.add('pe', lambda e: e.matmul(out, lhsT=lhsT, rhs=rhs, start=start, stop=stop), reads, writes)

        def tr(out, in_, idn, reads, writes):
            P.add('pe', lambda e: e.transpose(out=out, in_=in_, identity=idn), reads, writes)

        def act(out, in_, func, reads, writes, bias=None, scale=None, accum=None):
            kw = {}
            if bias is not None:
                kw['bias'] = bias
            if scale is not None:
                kw['scale'] = scale
            if accum is not None:
                kw['accum_out'] = accum
            P.add('act', lambda e: e.activation(out=out, in_=in_, func=func, **kw), reads, writes)

        def tt(out, in0, in1, op, reads, writes, eng='dve'):
            P.add(eng, lambda e: e.tensor_tensor(out=out, in0=in0, in1=in1, op=op), reads, writes)

        def ts(out, in0, s1, s2, op0, op1, reads, writes, eng='dve'):
            if op1 is None:
                P.add(eng, lambda e: e.tensor_scalar(out=out, in0=in0, scalar1=s1, scalar2=None, op0=op0), reads, writes)
            else:
                P.add(eng, lambda e: e.tensor_scalar(out=out, in0=in0, scalar1=s1, scalar2=s2, op0=op0, op1=op1), reads, writes)

        def stt(out, in0, scalar, in1, op0, op1, reads, writes, eng='dve'):
            P.add(eng, lambda e: e.scalar_tensor_tensor(out=out, in0=in0, scalar=scalar, in1=in1, op0=op0, op1=op1), reads, writes)

        def cpy(out, in_, reads, writes, eng='dve'):
            if eng == 'act':
                P.add(eng, lambda e: e.activation(out=out, in_=in_, func=AF.Copy), reads, writes)
            else:
                P.add(eng, lambda e: e.tensor_copy(out=out, in_=in_), reads, writes)

        def memset(ap, val, writes, eng='dve'):
            P.add(eng, lambda e: e.memset(ap, val), (), writes)

        NBLK = len(plan)
        wstate = dict(next_load=0, next_use=0, total=0)

        def w_issue():
            g = wstate['next_load']
            if g >= wstate['total']:
                return
            i = g % NBLK
            s = g % NSLOT
            name, l, b, kc, bw = plan[i]
            n = 128 * kc * bw
            src = bass.AP(tensor=wsc_t, offset=offs[i], ap=[[kc * bw, 128], [1, kc * bw]])
            dma('sp', ws[:, s, 0:kc * bw], src, ['wsc'], [('ws', s)], ('ws', s))
            wstate['next_load'] += 1

        def w_next(expect):
            g = wstate['next_use']
            i = g % NBLK
            assert plan[i][0] == expect, (plan[i], expect)
            while wstate['next_load'] < min(g + NSLOT, wstate['total']):
                w_issue()
            wstate['next_use'] += 1
            s = g % NSLOT
            kc, bw = plan[i][3], plan[i][4]
            return s, ws[:, s, 0:kc * bw].rearrange("p (k n) -> p k n", k=kc)

        for wi in range(NWP):
            for r0 in range(0, wprows, 1024):
                r1 = min(r0 + 1024, wprows)
                dma('pool', wsc_d[wi * wprows + r0:wi * wprows + r1, :], wall_ds[wi][r0:r1, :], [], ['wsc'], 'cast')
        dma('sp', cm[:], cm_d, [], ['cm'], 'c0')
        dma('sp', cmb, cmb_d, [], ['cmb'], 'c0')
        dma('sp', gngT[:], gngT_d, [], ['gngT'], 'c0')
        dma('sp', cT[:], cT_d, [], ['cT'], 'c0')
        dma('sp', adab[:], adab_d, [], ['adab'], 'c0')
        dma('sp', normg[:], normg_d, [], ['normg'], 'c0')
        dma('sp', poolw32, poolw_d, [], ['poolw32'], 'c0')
        dma('sp', poolv[:], poolv_d, [], ['poolv'], 'c0')
        dma('sp', gqk[:], gqk_d, [], ['gqk'], 'c0')
        cpy(ident[:], cm_ident, ['cmb'], ['ident'])
        cpy(ones[:], cm_ones, ['cmb'], ['ones'])
        cpy(bdiag[:], cm_bdiag, ['cmb'], ['bdiag'])
        cpy(cmask[:], cm_cmask.rearrange("p (h n) -> p h n", h=4), ['cmb'], ['cmask'])
        cpy(poolw[:], poolw32, ['poolw32'], ['poolw'])
        ts(gqk[:, 0:1], gqk[:, 0:1], 0.125, None, ALU.mult, None, ['gqk'], ['gqk'])

        rb = tmpf[0:16, 0, 0:512]
        rb2 = tmpf[0:16, 1, 0:512]
        dma('sp', rb[:, 0:320], relb_d, [], ['rb'], 'c2')
        cpy(rb2[:, 0:320], rb[:, 319:320].to_broadcast([16, 320]), ['rb'], ['rb2'])
        dma('sp', rext_t.ap()[:, 0:320], rb[:, 0:320], ['rb'], ['rext'], 'c3')
        dma('sp', rext_t.ap()[:, 320:640], rb2[:, 0:320], ['rb2'], ['rext'], 'c3')
        bias32 = scr[:, 8192:8192 + 2 * 7 * 1024].bitcast(F32).rearrange("p (t h q) -> p t h q", t=7, h=16)
        for p in range(128):
            kl = p % 64
            if p < 64:
                t0, d0 = 0, 0
            else:
                t0, d0 = 1, 0
            ntile = 7 - t0
            src = bass.AP(tensor=rext_t, offset=64 * d0 + 63 - kl, ap=[[0, 1], [64, ntile], [640, 16], [1, 64]])
            q = 'sp' if p % 2 == 0 else 'pool'
            dma(q, bias32[p:p + 1, t0:7, :, :], src, ['rext'], ['bias32'], 'c4')
        memset(bias32[64:128, 0, :, :], 0.0, ['bias32'])
        cpy(bias[:], bias32.rearrange("p t h q -> p t (h q)"), ['bias32'], ['bias'])

        act(cact[:], cT[:], AF.Silu, ['cT'], ['cact'])
        aws = scr[:, 8192:16384].rearrange("p (s f) -> p s f", s=2)
        P.seed([('aws', 0), ('aws', 1)], ['bias32'])
        gi = 0
        for l in range(4):
            for b in range(18):
                s = gi % 2
                gi += 1
                wv = aws[:, s, :].rearrange("p (k n) -> p k n", k=8)
                dma('pool', wv, adaw_d[l].rearrange("(k p) n -> p k n", p=128)[:, :, 512 * b:512 * b + 512],
                    [], [('aws', s)], ('aws', s))
                for j in range(4):
                    ncn = b * 4 + j
                    bk = bankA()
                    for kc in range(8):
                        mm(ps[:, bk, 0:NS], wv[:, kc, 128 * j:128 * j + 128], cact[:, kc, :], kc == 0, kc == 7,
                           [('aws', s), 'cact'], [('ps', bk)])
                    act(mod[:, l, ncn, :], ps[:, bk, 0:NS], AF.Identity, [('ps', bk), 'adab'], ['mod'],
                        bias=adab[:, l, ncn:ncn + 1])
        for l in range(4):
            for sub in range(3):
                scj = 1 + 3 * sub
                gj = 2 + 3 * sub
                ts(Atab[:, l, sub, :, :], mod[:, l, scj * 8:scj * 8 + 8, :], 1.0, None, ALU.add, None, ['mod'], ['Atab'])
                tt(Atab[:, l, sub, :, :], Atab[:, l, sub, :, :],
                   normg[:, l, sub, :].rearrange("p (c o) -> p c o", o=1).to_broadcast([128, NCH, NS]), ALU.mult,
                   ['Atab', 'normg'], ['Atab'])
                ts(Gtab[:, l, sub, :, :], mod[:, l, gj * 8:gj * 8 + 8, :], 1.0 if sub == 1 else 0.5, None, ALU.mult, None,
                   ['mod'], ['Gtab'])
                cpy(Btab[:, l, sub, :, :], mod[:, l, (3 * sub) * 8:(3 * sub) * 8 + 8, :], ['mod'], ['Btab'])
        tt(poolA[:, 0, :, :], Gtab[:, 1, 1, :, :], poolv[:, 1, :].rearrange("p (c o) -> p c o", o=1).to_broadcast([128, NCH, NS]),
           ALU.mult, ['Gtab', 'poolv'], ['poolA'])
        tt(poolA[:, 1, :, :], poolA[:, 0, :, :], poolv[:, 0, :].rearrange("p (c o) -> p c o", o=1).to_broadcast([128, NCH, NS]),
           ALU.mult, ['poolA', 'poolv'], ['poolA'])

        scr_keys = [[('aws', 0), ('aws', 1), 'bias32', 'mod', 'poolw32', 'cmb']]

        def scr_phase(new_keys):
            P.seed(new_keys, scr_keys[0])
            scr_keys[0] = list(new_keys)

        XK = [('x', c) for c in range(NCH)]
        HK = [('h', c) for c in range(NCH)]

        def norm(l, sub, segs, T, out_ap_fn, out_keys):
            bk = bankA()
            for c in range(NCH):
                act(sq[:, c % 2, 0:T], xT[:, c, 0:T], AF.Square, [('x', c)], [('sq', c % 2)])
                mm(ps[:, bk, 0:T], ones[:], sq[:, c % 2, 0:T], c == 0, c == NCH - 1, [('sq', c % 2), 'ones'], [('ps', bk)])
            act(rstd[:, 0:T], ps[:, bk, 0:T], AF.Sqrt, [('ps', bk)], ['rstd'], bias=EPS, scale=1.0 / D)
            P.add('dve', lambda e: e.reciprocal(out=rstd[:, 0:T], in_=rstd[:, 0:T]), ['rstd'], ['rstd'])
            for c in range(NCH):
                ti = tmp32()
                for (s, c0, c1) in segs:
                    stt(tmpf[:, ti, c0:c1], xT[:, c, c0:c1], Atab[:, l, sub, c, s:s + 1], rstd[:, c0:c1], ALU.mult, ALU.mult,
                        [('x', c), 'rstd', 'Atab'], [('tmp', ti)])
                    act(out_ap_fn(c, c0, c1), tmpf[:, ti, c0:c1], AF.Identity, [('tmp', ti), 'Btab'], [out_keys[c]],
                        bias=Btab[:, l, sub, c, s:s + 1])

        def residual(bk, n, segs, gate_fn, extra_reads=()):
            for (s, c0, c1) in segs:
                stt(xT[:, n, c0:c1], ps[:, bk, c0:c1], gate_fn(n, s), xT[:, n, c0:c1], ALU.mult, ALU.add,
                    [('ps', bk), ('x', n), 'Gtab'] + list(extra_reads), [('x', n)])

        def proj_fm(wname, nblocks, KC, BW, rhs_fn, rhs_keys, T, consumer):
            nper = BW // 128
            for b in range(nblocks):
                s, wv = w_next(wname)
                for j in range(nper):
                    bk = bankA()
                    for kc in range(KC):
                        mm(ps[:, bk, 0:T], wv[:, kc, 128 * j:128 * j + 128], rhs_fn(kc), kc == 0, kc == KC - 1,
                           [('ws', s)] + list(rhs_keys), [('ps', bk)])
                    consumer(b * nper + j, bk)

        def ffn(l, which, segs, T):
            sub = 0 if which == 1 else 2
            norm(l, sub, segs, T, lambda c, c0, c1: hT[:, c, c0:c1], HK)
            hid = scr[:, 0:22 * TILE].rearrange("p (c n) -> p c n", c=22)
            hk = [('hid', c) for c in range(22)]
            scr_phase(hk)
            wname = 'ffn%d_in' % which
            for b in range(11):
                s, wv = w_next(wname)
                for j in range(2):
                    ci = 2 * b + j
                    bg = bankA()
                    for kc in range(8):
                        mm(ps[:, bg, 0:T], wv[:, kc, 128 * j:128 * j + 128], hT[:, kc, 0:T], kc == 0, kc == 7,
                           [('ws', s)] + HK, [('ps', bg)])
                    bu = bankA()
                    for kc in range(8):
                        mm(ps[:, bu, 0:T], wv[:, kc, 256 + 128 * j:256 + 128 * j + 128], hT[:, kc, 0:T], kc == 0, kc == 7,
                           [('ws', s)] + HK, [('ps', bu)])
                    ti = tmp32()
                    act(tmpf[:, ti, 0:T], ps[:, bg, 0:T], AF.Silu, [('ps', bg)], [('tmp', ti)])
                    tt(hid[:, ci, 0:T], tmpf[:, ti, 0:T], ps[:, bu, 0:T], ALU.mult, [('tmp', ti), ('ps', bu)], [('hid', ci)])
            proj_fm('ffn%d_out' % which, 8, 22, 128, lambda kc: hid[:, kc, 0:T], hk, T,
                    lambda n, bk: residual(bk, n, segs, lambda n_, s_: Gtab[:, l, sub, n_, s_:s_ + 1]))

        def retention(l, j, segs, T, blocks, tinfo):
            norm(l, 1, segs, T, lambda c, c0, c1: hT[:, c, c0:c1], HK)
            nb = len(blocks)
            gz = scr[:, 0:8192].rearrange("p (b n) -> p b n", b=4)
            qk = scr[:, 8192:16384].rearrange("p (c n) -> p c n", c=16)
            vv = scr[:, 16384:24576].rearrange("p (b n) -> p b n", b=4)
            gk = [('gz', b) for b in range(4)]
            qkk = [('qk', b) for b in range(4)]
            vk = [('v', b) for b in range(4)]
            scr_phase(gk + qkk + vk)
            bl_of = blocks
            def qk_block(b, which):
                s, wv = w_next('ret_in')
                for hh in range(2):
                    h = 2 * (b % 2) + hh
                    b1 = bankA()
                    for kc in range(8):
                        mm(ps[:, b1, 0:T], wv[:, kc, 256 * hh:256 * hh + 128], hT[:, kc, 0:T], kc == 0, kc == 7,
                           [('ws', s)] + HK, [('ps', b1)])
                    b2 = bankA()
                    for kc in range(8):
                        mm(ps[:, b2, 0:T], wv[:, kc, 256 * hh + 128:256 * hh + 256], hT[:, kc, 0:T], kc == 0, kc == 7,
                           [('ws', s)] + HK, [('ps', b2)])
                    dec = qdec if which == 0 else kdec
                    t1, t2, t3, t4 = tmp32(), tmp32(), tmp32(), tmp32()
                    tt(tmpf[:, t1, 0:T], ps[:, b1, 0:T], cs[:, 0, 0:T], ALU.mult, [('ps', b1), 'cs'], [('tmp', t1)])
                    tt(tmpf[:, t2, 0:T], ps[:, b2, 0:T], cs[:, 1, 0:T], ALU.mult, [('ps', b2), 'cs'], [('tmp', t2)])
                    tt(tmpf[:, t3, 0:T], ps[:, b1, 0:T], cs[:, 1, 0:T], ALU.mult, [('ps', b1), 'cs'], [('tmp', t3)])
                    tt(tmpf[:, t4, 0:T], ps[:, b2, 0:T], cs[:, 0, 0:T], ALU.mult, [('ps', b2), 'cs'], [('tmp', t4)])
                    tt(tmpf[:, t1, 0:T], tmpf[:, t1, 0:T], tmpf[:, t2, 0:T], ALU.subtract, [('tmp', t1), ('tmp', t2)], [('tmp', t1)])
                    tt(tmpf[:, t3, 0:T], tmpf[:, t3, 0:T], tmpf[:, t4, 0:T], ALU.add, [('tmp', t3), ('tmp', t4)], [('tmp', t3)])
                    for bi, (c0, L, s_, fs, ls, si) in enumerate(bl_of):
                        for half, tsrc in ((0, t1), (1, t3)):
                            ch = 8 * which + 2 * h + half
                            tt(qk[:, ch, c0:c0 + L], tmpf[:, tsrc, c0:c0 + L], dec[:, h, 0:L], ALU.mult,
                               [('tmp', tsrc), 'cm'], [('qk', bi)])
            for b in range(2):
                qk_block(b, 0)
            for b in range(2):
                qk_block(b, 1)
            for b in range(4):
                s, wv = w_next('ret_in')
                for bi, (c0, L, s_, fs, ls, si) in enumerate(bl_of):
                    bk = bankA()
                    for kc in range(8):
                        mm(ps[0:L, bk, :], hT[:, kc, c0:c0 + L], wv[:, kc, :], kc == 0, kc == 7, [('ws', s)] + HK, [('ps', bk)])
                    cpy(vv[0:L, bi, 512 * b:512 * b + 512], ps[0:L, bk, :], [('ps', bk)], [('v', bi)])
            for b in range(4):
                s, wv = w_next('ret_in')
                for bi, (c0, L, s_, fs, ls, si) in enumerate(bl_of):
                    bk = bankA()
                    for kc in range(8):
                        mm(ps[0:L, bk, :], hT[:, kc, c0:c0 + L], wv[:, kc, :], kc == 0, kc == 7, [('ws', s)] + HK, [('ps', bk)])
                    act(gz[0:L, bi, 512 * b:512 * b + 512], ps[0:L, bk, :], AF.Silu, [('ps', bk)], [('gz', bi)])
            for bi, (c0, L, s_, fs, ls, si) in enumerate(bl_of):
                gam = [1.0 - 2.0 ** (-5.0 - h) for h in range(RET_H)]
                if fs:
                    if si['kind'] == 'prompt':
                        memset(Sst[:, j, :, :, :], 0.0, [('S', j)])
                    else:
                        for h_ in range(RET_H):
                            dma('pool', Sst[:, j, h_, :, :], sret_d[j, si['idx'], h_].rearrange("(a p) d -> p a d", p=128),
                                [], [('S', j)], 'Sld')

                if fs or bi == 0:
                    act(Sbf[:].rearrange("p h a d -> p (h a d)"), Sst[:, j, :, :, :].rearrange("p h a d -> p (h a d)"), AF.Copy,
                        [('S', j)], ['Sbf'])
                bt = bankA()
                for h in range(RET_H):
                    for half in range(2):
                        tr(psb[0:L, bt, 256 * h + 128 * half:256 * h + 128 * half + 128], qk[:, 8 + 2 * h + half, c0:c0 + L],
                           ident[:], [('qk', bi), 'ident'], [('ps', bt)])
                for h in range(RET_H):
                    act(ktok[0:L, 256 * h:256 * h + 256], psb[0:L, bt, 256 * h:256 * h + 256], AF.Copy, [('ps', bt)], ['ktok'],
                        scale=float(gam[h] ** L))
                bi_ = bankA()
                for h in range(RET_H):
                    for half in range(2):
                        mm(ps[0:L, bi_, 128 * h:128 * h + L], qk[:, 8 + 2 * h + half, c0:c0 + L], qk[:, 2 * h + half, c0:c0 + L],
                           half == 0, half == 1, [('qk', bi)], [('ps', bi_)])
                tt(innT[0:L, :, 0:L], ps[0:L, bi_, :].rearrange("p (h n) -> p h n", h=4)[:, :, 0:L], cmask[0:L, :, 0:L], ALU.mult,
                   [('ps', bi_), 'cmask'], ['innT'])
                obanks = []
                for h in range(RET_H):
                    bo = bankB()
                    mm(ps[0:L, bo, :], innT[0:L, h, 0:L], vv[0:L, bi, 512 * h:512 * h + 512], True, False,
                       ['innT', ('v', bi)], [('ps', bo)])
                    for half in range(2):
                        mm(ps[0:L, bo, :], qk[:, 2 * h + half, c0:c0 + L], Sbf[:, h, half, :], False, half == 1,
                           [('qk', bi), 'Sbf'], [('ps', bo)])
                    oi = h % 2
                    act(osb[0:L, oi, :], ps[0:L, bo, :], AF.Copy, [('ps', bo)], [('osb', oi)])
                    P.add('dve', lambda e, L=L, oi=oi, h=h: e.bn_stats(out=stat[0:L, h, 0:6], in_=osb[0:L, oi, :]),
                          [('osb', oi)], [('stat', h)])
                    P.add('dve', lambda e, L=L, h=h: e.bn_aggr(out=mv[0:L, h, :], in_=stat[0:L, h, 0:6]), [('stat', h)], [('mv', h)])
                    act(rsd[0:L, h:h + 1], mv[0:L, h, 1:2], AF.Sqrt, [('mv', h)], [('rsd', h)], bias=EPS, scale=1.0)
                    P.add('dve', lambda e, L=L, h=h: e.reciprocal(out=rsd[0:L, h:h + 1], in_=rsd[0:L, h:h + 1]), [('rsd', h)], [('rsd', h)])
                    ts(osb[0:L, oi, :], osb[0:L, oi, :], mv[0:L, h, 0:1], rsd[0:L, h:h + 1], ALU.subtract, ALU.mult,
                       [('osb', oi), ('mv', h), ('rsd', h)], [('osb', oi)])
                    tt(gz[0:L, bi, 512 * h:512 * h + 512], osb[0:L, oi, :], gz[0:L, bi, 512 * h:512 * h + 512], ALU.mult,
                       [('osb', oi), ('gz', bi)], [('gz', bi)])
                for h in range(RET_H):
                    for half in range(2):
                        bs = bankA()
                        mm(ps[:, bs, :], ktok[0:L, 256 * h + 128 * half:256 * h + 128 * half + 128], vv[0:L, bi, 512 * h:512 * h + 512],
                           True, True, ['ktok', ('v', bi)], [('ps', bs)])
                        stt(Sst[:, j, h, half, :], Sst[:, j, h, half, :], float(gam[h] ** L), ps[:, bs, :], ALU.mult, ALU.add,
                            [('S', j), ('ps', bs)], [('S', j)])
                        act(Sbf[:, h, half, :], Sst[:, j, h, half, :], AF.Copy, [('S', j)], ['Sbf'])
                if ls:
                    for h_ in range(RET_H):
                        dst = (retp_d if si['kind'] == 'prompt' else rets_d)[j, si['idx'], h_].rearrange("(a p) d -> p a d", p=128)
                        dma('pool', dst, Sst[:, j, h_, :, :], [('S', j)], [], 'Sst')
                for half8 in range(2):
                    bz = bankA()
                    for cc in range(8):
                        c = 8 * half8 + cc
                        tr(psb[:, bz, 128 * cc:128 * cc + L], gz[0:L, bi, 128 * c:128 * c + 128], ident[0:L, 0:L],
                           [('gz', bi), 'ident'], [('ps', bz)])
                    for cc in range(8):
                        c = 8 * half8 + cc
                        act(qk[:, c, c0:c0 + L], psb[:, bz, 128 * cc:128 * cc + L], AF.Copy, [('ps', bz), 'gngT'], [('qk', bi)],
                            scale=gngT[:, j, c:c + 1])
            proj_fm('ret_out', 4, 16, 256, lambda kc: qk[:, kc, 0:T], qkk, T,
                    lambda n, bk: residual(bk, n, segs, lambda n_, s_: Gtab[:, l, 1, n_, s_:s_ + 1]))

        def poolmix(l, segs, T, tinfo):
            hext = scr[:, 0:2 * NCH * 528].bitcast(F32).rearrange("p (c n) -> p c n", c=NCH)
            wsA = scr[:, 8448:8448 + 2 * NCH * 528].bitcast(F32).rearrange("p (c n) -> p c n", c=NCH)
            wsB6 = scr[:, 16896:16896 + 2 * 6 * 528].bitcast(F32).rearrange("p (c n) -> p c n", c=6)

            class _Sh:
                def __getitem__(self_, key):
                    p_, c_, n_ = key
                    return wsB6[p_, slice(c_.start - 2, c_.stop - 2), n_]
            wsB = _Sh()
            pooled = hT
            keys = ['hext', 'wsA', 'wsB']
            scr_phase(keys)
            norm(l, 1, segs, T, lambda c, c0, c1: hext[:, c, 16 + c0:16 + c1], ['hext'] * NCH)
            for (s, c0, c1) in segs:
                si = tinfo['seqinfo'][s]
                Ls = c1 - c0
                if si['first']:
                    if si['kind'] == 'prompt':
                        memset(hext[:, :, c0 + 1:c0 + 16], 0.0, ['hext'])
                    else:
                        dma('pool', hext[:, :, c0 + 1:c0 + 16], spool_d[si['idx']].rearrange("(c p) n -> p c n", p=128),
                            [], ['hext'], 'pl')
                else:
                    cpy(hext[:, :, c0 + 1:c0 + 16], poolcar[:], ['poolcar', 'hext'], ['hext'])
                memset(hext[:, :, c0:c0 + 1], 0.0, ['hext'])
                E = lambda buf, sh, c_lo, c_hi: buf[:, c_lo:c_hi, c0 + 16 - sh:c0 + 16 - sh + Ls]
                Ex = lambda buf, sh, c_lo, c_hi, lo: buf[:, c_lo:c_hi, c0 + lo - sh:c0 + 16 + Ls - sh]
                tt(Ex(wsA, 0, 0, 8, 2), Ex(hext, 0, 0, 8, 2), Ex(hext, 1, 0, 8, 2), ALU.add, ['hext'], ['wsA'])
                tt(Ex(wsB, 0, 2, 8, 4), Ex(wsA, 0, 2, 8, 4), Ex(wsA, 2, 2, 8, 4), ALU.add, ['wsA'], ['wsB'])
                tt(Ex(wsA, 0, 4, 8, 8), Ex(wsB, 0, 4, 8, 8), Ex(wsB, 4, 4, 8, 8), ALU.add, ['wsB', 'wsA'], ['wsA'])
                tt(Ex(wsB, 0, 6, 8, 16), Ex(wsA, 0, 6, 8, 16), Ex(wsA, 8, 6, 8, 16), ALU.add, ['wsA', 'wsB'], ['wsB'])
                srcs = [(wsA, 0, 2, 2.0), (wsB, 2, 4, 4.0), (wsA, 4, 6, 8.0), (wsB, 6, 8, 16.0)]
                if si['kind'] == 'prompt' and si['first']:
                    for (buf, lo, hi, w) in srcs:
                        tt(buf[:, lo:hi, c0 + 16:c0 + 32], buf[:, lo:hi, c0 + 16:c0 + 32], pratio[:, lo:hi, :], ALU.mult,
                           ['wsA', 'wsB', 'cm'], ['wsA', 'wsB'])
                for (buf, lo, hi, w) in srcs:
                    stt(pooled[:, lo:hi, c0:c1], E(buf, 0, lo, hi), 1.0 / w, E(hext, 0, lo, hi), ALU.mult, ALU.subtract,
                        ['wsA', 'wsB', 'hext'], HK)
                if si['last']:
                    dst = (poolp_d if si['kind'] == 'prompt' else pools_d)[si['idx']].rearrange("(c p) n -> p c n", p=128)
                    dma('pool', dst, hext[:, :, c0 + 1 + Ls:c0 + 16 + Ls], ['hext'], [], 'plo')
                else:
                    cpy(poolcar[:], hext[:, :, c0 + 1 + Ls:c0 + 16 + Ls], ['hext'], ['poolcar'])
            for g in range(4):
                for ec in range(2):
                    n = 2 * g + ec
                    bk = bankA()
                    for cc in range(2):
                        mm(ps[:, bk, 0:T], poolw[:, g, cc, 128 * ec:128 * ec + 128], pooled[:, 2 * g + cc, 0:T], cc == 0, cc == 1,
                           ['poolw'] + HK, [('ps', bk)])
                    for (s, c0, c1) in segs:
                        ti = tmp32()
                        act(tmpf[:, ti, c0:c1], ps[:, bk, c0:c1], AF.Identity, [('ps', bk), 'poolA'], [('tmp', ti)],
                            bias=poolA[:, 1, n, s:s + 1], scale=poolA[:, 0, n, s:s + 1])
                        tt(xT[:, n, c0:c1], xT[:, n, c0:c1], tmpf[:, ti, c0:c1], ALU.add, [('tmp', ti), ('x', n)], [('x', n)])

        def attention(l, segs, T, blocks, tinfo):
            astop = cfg.att_stop if tinfo['kind'] == 'prompt' else 99
            norm(l, 1, segs, T, lambda c, c0, c1: hT[:, c, c0:c1], HK)
            qT = scr[:, 0:4096].rearrange("p (c n) -> p c n", c=8)
            kT = scr[:, 4096:12288].rearrange("p (c n) -> p c n", c=8)
            V = scr[:, 12288:12288 + 8 * 1152].rearrange("p (b h d) -> p b h d", b=8, h=16)
            PT = scr[:, 21504:21504 + 5 * 512].rearrange("p (t n) -> p t n", t=5)
            QK = [('qT', i) for i in range(8)]
            qTo = Sbf[:].rearrange("p h a d -> p (h a) d")
            keys = QK + ['kTp', 'kTc', 'Vp', 'Vc'] + [('PT', i) for i in range(5)]
            scr_phase(keys)
            kind = tinfo['kind']
            for b_ in range(8):
                memset(V[:, b_, :, 64:72], 1.0, ['Vp', 'Vc'])
            def qk_consume(which):
                def f(n, bk):
                    nn = n % 8
                    ti = tmp32()
                    cpy(tmpf[:, ti, 0:T], ps[:, bk, 0:T], [('ps', bk)], [('tmp', ti)])
                    act(sq[:, nn % 2, 0:T], tmpf[:, ti, 0:T], AF.Square, [('tmp', ti)], [('sq', nn % 2)])
                    b2 = bankA()
                    mm(ps[:, b2, 0:T], bdiag[:], sq[:, nn % 2, 0:T], True, True, [('sq', nn % 2), 'bdiag'], [('ps', b2)])
                    t2 = tmp32()
                    act(tmpf[:, t2, 0:T], ps[:, b2, 0:T], AF.Sqrt, [('ps', b2)], [('tmp', t2)], bias=EPS, scale=1.0 / 64)
                    P.add('dve', lambda e, t2=t2: e.reciprocal(out=tmpf[:, t2, 0:T], in_=tmpf[:, t2, 0:T]), [('tmp', t2)], [('tmp', t2)])
                    if which == 0:
                        stt(qT[:, nn, 0:T], tmpf[:, ti, 0:T], gqk[:, 0:1], tmpf[:, t2, 0:T], ALU.mult, ALU.mult,
                            [('tmp', ti), ('tmp', t2), 'gqk'], QK)
                        cpy(qTo[64:128, nn, 0:T], qT[64:128, nn, 0:T], QK, ['Sbf'])
                        memset(qTo[0:64, nn, 0:T], 0.0, ['Sbf'])
                        memset(qT[64:128, nn, 0:T], 0.0, QK)
                    else:
                        stt(tmpf[:, ti, 0:T], tmpf[:, ti, 0:T], gqk[:, 1:2], tmpf[:, t2, 0:T], ALU.mult, ALU.mult,
                            [('tmp', ti), ('tmp', t2), 'gqk'], [('tmp', ti)])
                        cpy(kT[:, nn, 512:512 + T], tmpf[:, ti, 0:T], [('tmp', ti)], ['kTc'], eng='act')
                return f
            if astop <= 0:
                return
            proj_fm('att_qkv', 2, 8, 512, lambda kc: hT[:, kc, 0:T], HK, T, qk_consume(0))
            if astop <= 0.5:
                return
            proj_fm('att_qkv', 2, 8, 512, lambda kc: hT[:, kc, 0:T], HK, T, qk_consume(1))
            if astop <= 1:
                return
            for b in range(2):
                s, wv = w_next('att_qkv')
                for bi, (c0, L, s_, fs, ls, si) in enumerate(blocks):
                    bk = bankA()
                    for kc in range(8):
                        mm(ps[0:L, bk, :], hT[:, kc, c0:c0 + L], wv[:, kc, :], kc == 0, kc == 7, [('ws', s)] + HK, [('ps', bk)])
                    cpy(V[0:L, 4 + bi, 8 * b:8 * b + 8, 0:64], ps[0:L, bk, :].rearrange("p (h d) -> p h d", h=8), [('ps', bk)], ['Vc'])
            for bi, (c0, L, s_, fs, ls, si) in enumerate(blocks):
                if si['kind'] == 'sample':
                    dma('pool', ks_d[si['idx']].rearrange("(c p) n -> p c n", p=128), kT[:, :, 512 + c0:512 + c0 + L], ['kTc'], [], 'plo')
                    dma('pool', vs_d[si['idx']].rearrange("t (h d) -> t h d", h=16), V[0:L, 4 + bi, :, 0:64], ['Vc'], [], 'plo')
                elif tinfo['tile'] == NT - 1:
                    if bi == 0:
                        dma('pool', kp_d[si['idx']].rearrange("(c p) n -> p c n", p=128), kT[:, :, 512:1024], ['kTc'], [], 'plo')
                    dma('pool', vp_d[si['idx'], c0:c0 + L, :].rearrange("t (h d) -> t h d", h=16), V[0:L, 4 + bi, :, 0:64], ['Vc'], [], 'plo')
            if astop <= 2:
                return
            def attend(qc0, own_col, own_blk, tiles):
                obk = [bankB(), bankB(), bankB()]
                nt = len(tiles)
                for hg in range(2):
                    for ti_, (kcol, nk, vblk, bt, mlow) in enumerate(tiles):
                        bk = bankA()
                        for hh in range(8):
                            h = 8 * hg + hh
                            pp = 64 * (h % 2)
                            qsrc = qT if h % 2 == 0 else qTo
                            mm(ps[0:nk, bk, 64 * hh:64 * hh + 64], kT[:, h // 2, kcol:kcol + nk], qsrc[:, h // 2, qc0:qc0 + 64],
                               True, True, ['kTp', 'kTc', ('qT', qc0 // 64), 'Sbf'], [('ps', bk)])
                        t3 = tmp32()
                        tt(tmpf[0:nk, t3, :], ps[0:nk, bk, :], bias[0:nk, bt, 512 * hg:512 * hg + 512], ALU.add,
                           [('ps', bk), 'bias'], [('tmp', t3)])
                        pi = ti_
                        act(PT[0:nk, pi, :], tmpf[0:nk, t3, :], AF.Exp, [('tmp', t3)], [('PT', pi)])
                        if mlow:
                            memset(PT[0:64, pi, :], 0.0, [('PT', pi)])
                    if astop <= 4:
                        continue
                    for hh in range(8):
                        h = 8 * hg + hh
                        ob = obk[h // 7]
                        oc = 72 * (h % 7)
                        for ti_, (kcol, nk, vblk, bt, mlow) in enumerate(tiles):
                            pi = ti_
                            mm(ps[0:64, ob, oc:oc + 72], PT[0:nk, pi, 64 * hh:64 * hh + 64], V[0:nk, vblk, h, :], ti_ == 0, ti_ == nt - 1,
                               [('PT', pi), 'Vp', 'Vc'], [('ps', ob)])
                if astop <= 5:
                    return
                osv = osb[0:64, :, :].rearrange("p a n -> p (a n)")
                for g3 in range(3):
                    nh = 7 if g3 < 2 else 2
                    pv = ps[0:64, obk[g3], 0:72 * nh].rearrange("p (h d) -> p h d", h=nh)
                    P.add('dve', lambda e, pv=pv, g3=g3, nh=nh: e.reciprocal(out=rec[0:64, 7 * g3:7 * g3 + nh], in_=pv[:, :, 64]),
                          [('ps', obk[g3])], [('rec', g3)])
                    tt(osv[:, 448 * g3:448 * g3 + 64 * nh].rearrange("p (h d) -> p h d", h=nh), pv[:, :, 0:64],
                       rec[0:64, 7 * g3:7 * g3 + nh].rearrange("p (h o) -> p h o", o=1).to_broadcast([64, nh, 64]), ALU.mult,
                       [('ps', obk[g3]), ('rec', g3)], [('osb', 0), ('osb', 1)])
                if astop <= 6:
                    return
                ob16 = ktok[0:64, :]
                cpy(ob16, osv, [('osb', 0), ('osb', 1)], ['ktok'], eng='act')
                bz = bankA()
                for c in range(8):
                    tr(psb[:, bz, 64 * c:64 * c + 64], ob16[:, 128 * c:128 * c + 128], ident[0:64, 0:64], ['ktok', 'ident'], [('ps', bz)])
                cpy(qT[:, :, qc0:qc0 + 64], psb[:, bz, 0:512].rearrange("p (c n) -> p c n", c=8), [('ps', bz)], [('qT', qc0 // 64)])

            CONSTT = 6
            if kind == 'prompt':
                tile_i = tinfo['tile']
                if tile_i > 0:
                    dma('pool', kT[:, :, 0:512], carK_d, ['carK'], ['kTp'], ('aws', 0))
                    dma('pool', V[:, 0:4, :, :].rearrange("p b h d -> p b (h d)"), carV_d, ['carV'], ['Vp'], ('aws', 0))
                else:
                    memset(kT[:, :, 448:512], 0.0, ['kTp'])
                    memset(V[:, 3, :, 0:64], 0.0, ['Vp'])
                for qc in range(T // 64):
                    n = tile_i * 8 + qc
                    own = 512 + 64 * qc
                    tiles = []
                    if qc % 2 == 0:
                        for i, bt in enumerate([CONSTT, CONSTT, 4, 2]):
                            m0 = n - 8 + 2 * i
                            if m0 >= 0:
                                tiles.append((own - 512 + 128 * i, 128, qc // 2 + i, bt, False))
                        tiles.append((own, 64, 4 + qc // 2, 0, False))
                    else:
                        for i, bt in enumerate([CONSTT, CONSTT, 5, 3, 1]):
                            m0 = n - 9 + 2 * i
                            if m0 + 1 >= 0:
                                col = own - 576 + 128 * i
                                tiles.append((col, 128, col // 128, bt, i == 0))
                    attend(64 * qc, own, None, tiles)
                if tile_i < NT - 1:
                    dma('pool', carK_d, kT[:, :, 512:1024], ['kTc'], ['carK'], ('aws', 1))
                    dma('pool', carV_d, V[:, 4:8, :, :].rearrange("p b h d -> p b (h d)"), ['Vc'], ['carV'], ('aws', 1))
            else:
                for bi, (c0, L, s_, fs, ls, si) in enumerate(blocks):
                    dma('pool', kT[:, :, 0:512], ckT_d[si['idx']].rearrange("(c p) n -> p c n", p=128), [], ['kTp'], ('aws', 0))
                    for b_ in range(4):
                        dma('pool', V[:, b_, :, 0:64], cv_d[si['idx'], 128 * b_:128 * b_ + 128, :].rearrange("p (h d) -> p h d", h=16),
                            [], ['Vp'], ('aws', 0))
                    tiles = [(128 * i, 128, i, bt, False) for i, bt in enumerate([CONSTT, CONSTT, 4, 2])]
                    tiles.append((512 + c0, 64, 4 + bi, 0, False))
                    if astop > 3:
                        attend(c0, 512 + c0, None, tiles)
            if astop < 99:
                return
            proj_fm('att_o', 2, 8, 512, lambda kc: qT[:, kc, 0:T], QK, T,
                    lambda n, bk: residual(bk, n, segs, lambda n_, s_: Gtab[:, l, 1, n_, s_:s_ + 1]))

        tiles = []
        if cfg.sample:
            seqinfo = {NSEQ + i: dict(kind='sample', idx=i, first=True, last=True) for i in range(2)}
            tiles.append(dict(kind='sample', tile=0, T=128, segs=[(NSEQ, 0, 64), (NSEQ + 1, 64, 128)], seqinfo=seqinfo,
                              blocks=[(0, 64, NSEQ, True, True, seqinfo[NSEQ]), (64, 64, NSEQ + 1, True, True, seqinfo[NSEQ + 1])],
                              pos0=SEQ))
        for sq_ in range(NSEQ):
            for t in range(NT):
                si = dict(kind='prompt', idx=sq_, first=(t == 0), last=(t == NT - 1))
                tiles.append(dict(kind='prompt', tile=t, T=TILE, segs=[(sq_, 0, TILE)], seqinfo={sq_: si},
                                  blocks=[(128 * b, 128, sq_, t == 0 and b == 0, t == NT - 1 and b == 3, si) for b in range(4)],
                                  pos0=t * TILE))
        wstate['total'] = NBLK * len(tiles)
        stage = [0]

        for tinfo in tiles:
            T = tinfo['T']
            segs = tinfo['segs']
            if tinfo['kind'] == 'prompt':
                s0 = segs[0][0]
                src = xT_d[s0].rearrange("(c p) n -> p c n", p=128)[:, :, tinfo['pos0']:tinfo['pos0'] + T]
                dma('pool', xT[:, :, 0:T], src, [], XK, 'xin')
                dma('pool', cs[:, :, 0:T], cf_d[:, :, tinfo['pos0']:tinfo['pos0'] + T], [], ['cs'], 'xin')
            else:
                for (s, c0, c1) in segs:
                    dma('pool', xT[:, :, c0:c1], xsT_d[s - NSEQ].rearrange("(c p) n -> p c n", p=128), [], XK, 'xin')
                    dma('pool', cs[:, :, c0:c1], cf_d[:, :, SEQ:SEQ + 64], [], ['cs'], 'xin')
            for l in range(4):
                for sub_ in range(3):
                    if stage[0] >= cfg.limit:
                        continue
                    stage[0] += 1
                    if sub_ == 0:
                        ffn(l, 1, segs, T)
                    elif sub_ == 2:
                        ffn(l, 2, segs, T)
                    elif l % 3 == 0:
                        retention(l, l // 3, segs, T, tinfo['blocks'], tinfo)
                    elif l % 3 == 1:
                        poolmix(l, segs, T, tinfo)
                    else:
                        attention(l, segs, T, tinfo['blocks'], tinfo)
            if tinfo['kind'] == 'prompt':
                s0 = segs[0][0]
                dst = yT_d[s0].rearrange("(c p) n -> p c n", p=128)[:, :, tinfo['pos0']:tinfo['pos0'] + T]
                dma('pool', dst, xT[:, :, 0:T], XK, [], 'xin')
            else:
                for (s, c0, c1) in segs:
                    dma('pool', ysT_d[s - NSEQ].rearrange("(c p) n -> p c n", p=128), xT[:, :, c0:c1], XK, [], 'xin')

        P.emit(nc, st)
    return nc


def host_consts(seq):
    half = 128
    inv = (10000.0 ** (-np.arange(half, dtype=np.float32) / np.float32(half))).astype(np.float32)
    pos = np.concatenate([np.arange(seq), 2048 + np.arange(64)]).astype(np.float32)
    ang = (pos[None, :] * inv[:, None]).astype(np.float32)
    cf = np.stack([np.cos(ang), np.sin(ang)], axis=1).astype(np.float32)
    gam = 1.0 - 2.0 ** (-5.0 - np.arange(4, dtype=np.float64))
    i = np.arange(128, dtype=np.float64)
    qdec = (gam[:, None] ** (i[None, :] + 1.0))
    kdec = (gam[:, None] ** (-(i[None, :] + 1.0))) * (256.0 ** -0.5)
    ident = np.eye(128)
    ones = np.ones((128, 128))
    bd = np.zeros((128, 128)); bd[0:64, 0:64] = 1; bd[64:, 64:] = 1
    cmask = (np.arange(128)[None, :] >= np.arange(128)[:, None]).astype(np.float64)
    cmask4 = np.tile(cmask[:, None, :], (1, 4, 1)).reshape(128, 512)
    ratio = np.zeros((8, 16))
    for g, w in enumerate((2, 4, 8, 16)):
        t = np.arange(16)
        ratio[2 * g:2 * g + 2, :] = (w / np.minimum(t + 1, w))[None, :]
    cm = np.concatenate([
        np.tile(qdec.reshape(1, 512), (128, 1)), np.tile(kdec.reshape(1, 512), (128, 1)),
        np.tile(ratio.reshape(1, 128), (128, 1))], axis=1).astype(np.float32)
    cmb = np.concatenate([ident, ones, bd, cmask4], axis=1).astype(np.float32)
    return cf, cm, cmb


def prep_shared(inputs, plan, offs, wtot, wrows, CW=8192):
    f = lambda a: np.asarray(a, dtype=np.float32)
    inp = {k: f(v) for k, v in inputs.items()}
    wall = np.zeros(wrows * CW, dtype=np.float32)
    for (name, l, b, kc, bw), o in zip(plan, offs):
        blk = host_block(inp, name, l, b, kc, bw)
        wall[o:o + blk.size] = blk
    sh = {}
    wl = wall.reshape(wrows, CW)
    for i in range(4):
        sh['wall%d' % i] = wl[i * (wrows // 4):(i + 1) * (wrows // 4)]
    sh['ada_w'] = np.ascontiguousarray(inp['ada_w'])
    sh['adabT'] = np.ascontiguousarray(inp['ada_b'].reshape(4, 72, 128).transpose(2, 0, 1))
    sh['normgT'] = np.ascontiguousarray(inp['norm_g'].reshape(4, 3, 8, 128).transpose(3, 0, 1, 2))
    sh['poolw'] = np.ascontiguousarray(inp['pool_w'][0].reshape(4, 2, 128, 256).transpose(2, 0, 1, 3))
    sh['poolv'] = np.ascontiguousarray(np.stack([inp['pool_b'][0].reshape(8, 128).T, inp['pool_scale'][0].reshape(8, 128).T], axis=1))
    sh['gqk'] = np.ascontiguousarray(np.stack([np.tile(inp['att_q_g'][0], 2), np.tile(inp['att_k_g'][0], 2)], axis=1))
    sh['gngT'] = np.ascontiguousarray(inp['ret_gn_g'].reshape(2, 16, 128).transpose(2, 0, 1))
    sh['relb'] = np.ascontiguousarray(inp['att_rel_bias'][0])
    return inp, sh


_CACHE = {}


def make_inmaps(inputs, cfg, ncores):
    plan, offs, wtot = weight_plan()
    CW = 8192
    wrows = (wtot + CW - 1) // CW
    wrows = ((wrows + 1023) // 1024) * 1024
    inp, sh = prep_shared(inputs, plan, offs, wtot, wrows)
    cf, cm, cmb = host_consts(cfg.seq)
    sh['cf'] = cf
    sh['cm'] = cm
    sh['cmb'] = cmb
    nq = cfg.nseq
    in_maps = []
    for c in range(ncores):
        m = dict(sh)
        ps_ = slice(nq * c, nq * c + nq)
        ss_ = slice(2 * c, 2 * c + 2)
        m['xT'] = np.ascontiguousarray(inp['x_prompt'][ps_, :cfg.seq].transpose(0, 2, 1))
        m['xsT'] = np.ascontiguousarray(inp['x_sample'][ss_].transpose(0, 2, 1))
        cc = np.concatenate([inp['c_prompt'][ps_], inp['c_sample'][ss_]], axis=0)
        m['cT'] = np.ascontiguousarray(cc.reshape(nq + 2, 8, 128).transpose(2, 1, 0))
        m['sret'] = np.ascontiguousarray(inp['state_ret'][:, ss_])
        m['spoolT'] = np.ascontiguousarray(inp['state_pool'][0, ss_].transpose(0, 2, 1))
        m['ckT'] = np.ascontiguousarray(inp['cache_att_k'][0, ss_].reshape(2, 512, 1024).transpose(0, 2, 1))
        m['cv'] = np.ascontiguousarray(inp['cache_att_v'][0, ss_].reshape(2, 512, 1024))
        in_maps.append(m)
    return in_maps


def gather(R, nprompt, nsample):
    cat = lambda k, ax: np.concatenate([r[k] for r in R], axis=ax)
    y_prompt = cat('yT', 0).transpose(0, 2, 1)
    y_sample = cat('ysT', 0).transpose(0, 2, 1)
    ret_p = cat('retp', 1)
    ret_s = cat('rets', 1)
    pool_p = cat('poolpT', 0).transpose(0, 2, 1)[None]
    pool_s = cat('poolsT', 0).transpose(0, 2, 1)[None]
    k_p = cat('kpT', 0).transpose(0, 2, 1).reshape(nprompt, 512, 16, 64)[None]
    v_p = cat('vp', 0).reshape(nprompt, 512, 16, 64)[None]
    k_s = cat('ksT', 0).transpose(0, 2, 1).reshape(nsample, 64, 16, 64)[None]
    v_s = cat('vs', 0).reshape(nsample, 64, 16, 64)[None]
    outs = (y_prompt, y_sample, ret_p, ret_s, pool_p, pool_s, k_p, v_p, k_s, v_s)
    return tuple(np.ascontiguousarray(o, dtype=np.float32) for o in outs)


def kernel(**inputs):
    try:
        import jax
        jax.config.update('jax_default_device', jax.devices('cpu')[0])
    except Exception:
        pass
    ncores = 8
    cfg = Cfg(nseq=4, seq=2048, sample=True)
    in_maps = make_inmaps(inputs, cfg, ncores)
    if 'nc' not in _CACHE:
        _CACHE['nc'] = build(cfg)
    res = run_bass_kernel_spmd(_CACHE['nc'], in_maps, core_ids=list(range(ncores)))
    return gather(res.results, 32, 16)
```

```python
import numpy as np
import concourse.bass as bass
import concourse.mybir as mybir
from concourse.bass_utils import run_bass_kernel_spmd
from contextlib import ExitStack

F32 = mybir.dt.float32
BF16 = mybir.dt.bfloat16
AF = mybir.ActivationFunctionType
ALU = mybir.AluOpType

D = 1024
DFF = 2816
NCH = 8
EPS = 1e-6
TILE = 512
NSLOT = 5
SLOT = 4096
RET_H = 4
ATT_H = 16


class Prog:
    def __init__(self):
        self.ops = []
        self.last_w = {}
        self.readers = {}
        self.dma_cnt = {}

    def seed(self, new_keys, old_keys):
        lw = set()
        rd = {}
        for k in old_keys:
            lw |= self.last_w.get(k, set())
            for e, i in self.readers.get(k, {}).items():
                if rd.get(e, -1) < i:
                    rd[e] = i
        for k in new_keys:
            self.last_w[k] = set(lw) | self.last_w.get(k, set())
            r = dict(self.readers.get(k, {}))
            for e, i in rd.items():
                if r.get(e, -1) < i:
                    r[e] = i
            self.readers[k] = r

    def add(self, eng, fn, reads=(), writes=(), dma=None):
        idx = len(self.ops)
        raw = set()
        oth = set()
        for r in reads:
            raw |= self.last_w.get(r, set())
        for w in writes:
            oth |= self.last_w.get(w, set())
            for e, ri in self.readers.get(w, {}).items():
                oth.add(ri)
        waits = []
        for d in raw | oth:
            Pd = self.ops[d]
            if Pd['dma'] is not None:
                waits.append(('dma', Pd['dma'], 16 * self.dma_cnt[Pd['dma']]))
            else:
                if Pd['eng'] == eng and eng == 'pe':
                    continue
                Pd['signal'] = True
                waits.append(('op', d))
        op = dict(eng=eng, fn=fn, dma=dma, signal=False, waits=waits)
        if dma is not None:
            self.dma_cnt[dma] = self.dma_cnt.get(dma, 0) + 1
        self.ops.append(op)
        ek = eng if dma is None else ('dma', dma)
        for r in reads:
            self.readers.setdefault(r, {})[ek] = idx
        for w in writes:
            self.last_w[w] = {idx}
            self.readers[w] = {}
        return idx

    def emit(self, nc, stack):
        engs = ['pe', 'act', 'dve', 'pool', 'sp']
        esem = {e: stack.enter_context(nc.semaphore('s_' + e)) for e in engs}
        dsem = {k: stack.enter_context(nc.semaphore('d_%d' % i)) for i, k in enumerate(self.dma_cnt)}
        cnt = {e: 0 for e in engs}
        for op in self.ops:
            if op['signal'] and op['dma'] is None:
                cnt[op['eng']] += 1
                op['sigval'] = cnt[op['eng']]
        per = {e: [] for e in engs}
        for op in self.ops:
            per[op['eng']].append(op)
        ops = self.ops
        dma_cnt = self.dma_cnt
        block = stack.enter_context(nc.Block())

        def run(eng_name, eng):
            waited = {}
            for op in per[eng_name]:
                need = {}
                for w in op['waits']:
                    if w[0] == 'dma':
                        key = ('d', w[1])
                        val = w[2]
                        sem = dsem[w[1]]
                    else:
                        Pd = ops[w[1]]
                        key = ('e', Pd['eng'])
                        val = Pd['sigval']
                        sem = esem[Pd['eng']]
                    if need.get(key, (None, 0))[1] < val:
                        need[key] = (sem, val)
                for key, (sem, val) in need.items():
                    if waited.get(key, 0) < val:
                        eng.wait_ge(sem, val)
                        waited[key] = val
                ins = op['fn'](eng)
                if op['dma'] is not None:
                    ins.then_inc(dsem[op['dma']], 16)
                elif op['signal']:
                    ins.then_inc(esem[eng_name], 1)
            if eng_name == 'sp':
                for k, c in dma_cnt.items():
                    eng.wait_ge(dsem[k], 16 * c)

        @block.tensor
        def _(e):
            run('pe', e)

        @block.scalar
        def _(e):
            run('act', e)

        @block.vector
        def _(e):
            run('dve', e)

        @block.gpsimd
        def _(e):
            run('pool', e)

        @block.sync
        def _(e):
            run('sp', e)


def weight_plan():
    plan = []

    def ffn(l, w):
        for b in range(11):
            plan.append(('ffn%d_in' % w, l, b, 8, 512))
        for b in range(8):
            plan.append(('ffn%d_out' % w, l, b, 22, 128))

    for l in range(4):
        ffn(l, 1)
        if l % 3 == 0:
            j = l // 3
            for b in range(12):
                plan.append(('ret_in', j, b, 8, 512))
            for b in range(4):
                plan.append(('ret_out', j, b, 16, 256))
        elif l % 3 == 2:
            for b in range(6):
                plan.append(('att_qkv', 0, b, 8, 512))
            for b in range(2):
                plan.append(('att_o', 0, b, 8, 512))
        ffn(l, 2)
    offs = []
    o = 0
    for (_, _, _, kc, bw) in plan:
        offs.append(o)
        o += 128 * kc * bw
    return plan, offs, o


def host_block(inputs, name, l, b, kc, bw):
    if name in ('ffn1_in', 'ffn2_in'):
        W = inputs[name[:4] + '_w_in'][l]
        cols = np.concatenate([np.arange(256 * b, 256 * b + 256), DFF + np.arange(256 * b, 256 * b + 256)])
        Wb = W[:, cols]
    elif name in ('ffn1_out', 'ffn2_out'):
        W = inputs[name[:4] + '_w_out'][l]
        Wb = W[:, 128 * b:128 * b + 128]
    elif name == 'ret_in':
        Wb = inputs['ret_w_in'][l][:, 512 * b:512 * b + 512]
    elif name == 'ret_out':
        Wb = inputs['ret_w_out'][l][:, 256 * b:256 * b + 256]
    elif name == 'att_qkv':
        Wb = inputs['att_w_qkv'][0][:, 512 * b:512 * b + 512]
    elif name == 'att_o':
        Wb = inputs['att_w_o'][0][:, 512 * b:512 * b + 512]
    return np.ascontiguousarray(Wb.reshape(kc, 128, bw).transpose(1, 0, 2)).reshape(-1)


class Cfg:
    def __init__(self, nseq=4, seq=2048, sample=True, dbg=False, limit=10 ** 9):
        self.limit = limit
        self.att_stop = 99
        self.nseq = nseq
        self.seq = seq
        self.sample = sample
        self.dbg = dbg
        self.ns = nseq + 2


def build(cfg):
    nc = bass.Bass("TRN2", target_bir_lowering=False)
    P = Prog()
    NS = cfg.ns
    NSEQ = cfg.nseq
    SEQ = cfg.seq
    NT = SEQ // TILE
    plan, offs, wtot = weight_plan()
    CW = 8192
    wrows = (wtot + CW - 1) // CW
    wrows = ((wrows + 1023) // 1024) * 1024

    def din(name, shape):
        return nc.dram_tensor(name, list(shape), F32, kind="ExternalInput")

    def dout(name, shape):
        return nc.dram_tensor(name, list(shape), F32, kind="ExternalOutput")

    xT_d = din("xT", [NSEQ, D, SEQ]).ap()
    xsT_d = din("xsT", [2, D, 64]).ap()
    cT_d = din("cT", [128, NCH, NS]).ap()
    sret_d = din("sret", [2, 2, RET_H, 256, 512]).ap()
    spool_d = din("spoolT", [2, D, 15]).ap()
    ckT_d = din("ckT", [2, D, 512]).ap()
    cv_d = din("cv", [2, 512, D]).ap()
    NWP = 4
    assert wrows % NWP == 0
    wprows = wrows // NWP
    wall_ds = [din("wall%d" % i, [wprows, CW]).ap() for i in range(NWP)]
    adaw_d = din("ada_w", [4, D, 9 * D]).ap()
    adab_d = din("adabT", [128, 4, 72]).ap()
    normg_d = din("normgT", [128, 4, 3, NCH]).ap()
    poolw_d = din("poolw", [128, 4, 2, 256]).ap()
    poolv_d = din("poolv", [128, 2, NCH]).ap()
    gqk_d = din("gqk", [128, 2]).ap()
    relb_d = din("relb", [16, 320]).ap()
    cf_d = din("cf", [128, 2, SEQ + 64]).ap()
    cm_d = din("cm", [128, 1152]).ap()
    cmb_d = din("cmb", [128, 896]).ap()
    gngT_d = din("gngT", [128, 2, 16]).ap()

    yT_d = dout("yT", [NSEQ, D, SEQ]).ap()
    ysT_d = dout("ysT", [2, D, 64]).ap()
    retp_d = dout("retp", [2, NSEQ, RET_H, 256, 512]).ap()
    rets_d = dout("rets", [2, 2, RET_H, 256, 512]).ap()
    poolp_d = dout("poolpT", [NSEQ, D, 15]).ap()
    pools_d = dout("poolsT", [2, D, 15]).ap()
    kp_d = dout("kpT", [NSEQ, D, 512]).ap()
    vp_d = dout("vp", [NSEQ, 512, D]).ap()
    ks_d = dout("ksT", [2, D, 64]).ap()
    vs_d = dout("vs", [2, 64, D]).ap()

    wsc_t = nc.dram_tensor("wsc", [wrows, CW], BF16, kind="Internal")
    wsc_d = wsc_t.ap()
    rext_t = nc.dram_tensor("rext", [16, 640], F32, kind="Internal")
    carK_d = nc.dram_tensor("carK", [128, NCH, 512], BF16, kind="Internal").ap()
    carV_d = nc.dram_tensor("carV", [128, 4, ATT_H * 72], BF16, kind="Internal").ap()

    st = ExitStack()
    with st:
        def sb(name, shape, dt):
            return st.enter_context(nc.sbuf_tensor(name, list(shape), dt))

        xT = sb("xT_s", [128, NCH, TILE], F32)
        hT = sb("hT_s", [128, NCH, TILE], BF16)
        SCRN = 24576
        scr = sb("scr", [128, SCRN], BF16)
        ws = sb("ws", [128, NSLOT, SLOT], BF16)
        Sst = sb("Sst", [128, 2, RET_H, 2, 512], F32)
        Sbf = sb("Sbf", [128, RET_H, 2, 512], BF16)
        rstd = sb("rstd", [128, TILE], F32)
        sq = sb("sq", [128, 2, TILE], BF16)
        tmpf = sb("tmpf", [128, 4, TILE], F32)
        Btab = sb("Btab", [128, 4, 3, NCH, NS], F32)
        Atab = sb("Atab", [128, 4, 3, NCH, NS], F32)
        Gtab = sb("Gtab", [128, 4, 3, NCH, NS], F32)
        cact = sb("cact", [128, NCH, NS], BF16)
        normg = sb("normg", [128, 4, 3, NCH], F32)
        poolw = sb("poolw_s", [128, 4, 2, 256], BF16)
        poolv = sb("poolv_s", [128, 2, NCH], F32)
        poolA = sb("poolA", [128, 2, NCH, NS], F32)
        poolcar = sb("poolcar", [128, NCH, 15], F32)
        gqk = sb("gqk_s", [128, 2], F32)
        gngT = sb("gngT_s", [128, 2, 16], F32)
        cm = sb("cm_s", [128, 1152], F32)
        mod = scr[:, 0:2 * 4 * 72 * NS].bitcast(F32).rearrange("p (l n s) -> p l n s", l=4, n=72)
        poolw32 = scr[:, 4096:4096 + 4096].bitcast(F32).rearrange("p (g c e) -> p g c e", g=4, c=2)
        cmb = scr[:, 22528:22528 + 1792].bitcast(F32)
        adab = scr[:, 3456:3456 + 576].bitcast(F32).rearrange("p (l n) -> p l n", l=4)
        cT = scr[:, 24320:24320 + 2 * NCH * NS].bitcast(F32).rearrange("p (k s) -> p k s", k=NCH)
        cs = sb("cs", [128, 2, TILE], F32)
        ident = sb("ident", [128, 128], BF16)
        ones = sb("ones", [128, 128], BF16)
        bdiag = sb("bdiag", [128, 128], BF16)
        cmask = sb("cmask", [128, 4, 128], BF16)
        bias = sb("bias", [128, 7, ATT_H * 64], BF16)
        ktok = sb("ktok", [128, 1024], BF16)
        innT = sb("innT", [128, 4, 128], BF16)
        stat = sb("stat", [128, 4, 8], F32)
        mv = sb("mv", [128, 4, 2], F32)
        rsd = sb("rsd", [128, 4], F32)
        rec = sb("rec", [128, 16], F32)
        osb = sb("osb", [128, 2, 512], F32)
        ps = st.enter_context(nc.psum_tensor("ps", [128, 8, 512], F32))
        psb = ps[:].bitcast(BF16)

        o = 0
        qdec = cm[:, o:o + 512].rearrange("p (h n) -> p h n", h=4); o += 512
        kdec = cm[:, o:o + 512].rearrange("p (h n) -> p h n", h=4); o += 512
        pratio = cm[:, o:o + 128].rearrange("p (c n) -> p c n", c=8); o += 128
        cm_ident = cmb[:, 0:128]
        cm_ones = cmb[:, 128:256]
        cm_bdiag = cmb[:, 256:384]
        cm_cmask = cmb[:, 384:896]

        rotA = [0]
        rotB = [0]

        def bankA():
            b = rotA[0] % 5
            rotA[0] += 1
            return b

        def bankB():
            b = 5 + rotB[0] % 3
            rotB[0] += 1
            return b

        tmpi = [0]

        def tmp32():
            i = tmpi[0] % 4
            tmpi[0] += 1
            return i

        def dma(q, out, in_, reads, writes, key):
            P.add(q, lambda e: e.dma_start(out=out, in_=in_), reads, writes, dma=(key, q))

        def mm(out, lhsT, rhs, start, stop, reads, writes):
            P.add('pe', lambda e: e.matmul(out, lhsT=lhsT, rhs=rhs, start=start, stop=stop), reads, writes)

        def tr(out, in_, idn, reads, writes):
            P.add('pe', lambda e: e.transpose(out=out, in_=in_, identity=idn), reads, writes)

        def act(out, in_, func, reads, writes, bias=None, scale=None, accum=None):
            kw = {}
            if bias is not None:
                kw['bias'] = bias
            if scale is not None:
                kw['scale'] = scale
            if accum is not None:
                kw['accum_out'] = accum
            P.add('act', lambda e: e.activation(out=out, in_=in_, func=func, **kw), reads, writes)

        def tt(out, in0, in1, op, reads, writes, eng='dve'):
            P.add(eng, lambda e: e.tensor_tensor(out=out, in0=in0, in1=in1, op=op), reads, writes)

        def ts(out, in0, s1, s2, op0, op1, reads, writes, eng='dve'):
            if op1 is None:
                P.add(eng, lambda e: e.tensor_scalar(out=out, in0=in0, scalar1=s1, scalar2=None, op0=op0), reads, writes)
            else:
                P.add(eng, lambda e: e.tensor_scalar(out=out, in0=in0, scalar1=s1, scalar2=s2, op0=op0, op1=op1), reads, writes)

        def stt(out, in0, scalar, in1, op0, op1, reads, writes, eng='dve'):
            P.add(eng, lambda e: e.scalar_tensor_tensor(out=out, in0=in0, scalar=scalar, in1=in1, op0=op0, op1=op1), reads, writes)

        def cpy(out, in_, reads, writes, eng='dve'):
            if eng == 'act':
                P.add(eng, lambda e: e.activation(out=out, in_=in_, func=AF.Copy), reads, writes)
            else:
                P.add(eng, lambda e: e.tensor_copy(out=out, in_=in_), reads, writes)

        def memset(ap, val, writes, eng='dve'):
            P.add(eng, lambda e: e.memset(ap, val), (), writes)

        NBLK = len(plan)
        wstate = dict(next_load=0, next_use=0, total=0)

        def w_issue():
            g = wstate['next_load']
            if g >= wstate['total']:
                return
            i = g % NBLK
            s = g % NSLOT
            name, l, b, kc, bw = plan[i]
            n = 128 * kc * bw
            src = bass.AP(tensor=wsc_t, offset=offs[i], ap=[[kc * bw, 128], [1, kc * bw]])
            dma('sp', ws[:, s, 0:kc * bw], src, ['wsc'], [('ws', s)], ('ws', s))
            wstate['next_load'] += 1

        def w_next(expect):
            g = wstate['next_use']
            i = g % NBLK
            assert plan[i][0] == expect, (plan[i], expect)
            while wstate['next_load'] < min(g + NSLOT, wstate['total']):
                w_issue()
            wstate['next_use'] += 1
            s = g % NSLOT
            kc, bw = plan[i][3], plan[i][4]
            return s, ws[:, s, 0:kc * bw].rearrange("p (k n) -> p k n", k=kc)

        for wi in range(NWP):
            for r0 in range(0, wprows, 1024):
                r1 = min(r0 + 1024, wprows)
                dma('pool', wsc_d[wi * wprows + r0:wi * wprows + r1, :], wall_ds[wi][r0:r1, :], [], ['wsc'], 'cast')
        dma('sp', cm[:], cm_d, [], ['cm'], 'c0')
        dma('sp', cmb, cmb_d, [], ['cmb'], 'c0')
        dma('sp', gngT[:], gngT_d, [], ['gngT'], 'c0')
        dma('sp', cT, cT_d, [], ['cT'], 'c0')
        dma('sp', adab, adab_d, [], ['adab'], 'c0')
        dma('sp', normg[:], normg_d, [], ['normg'], 'c0')
        dma('sp', poolw32, poolw_d, [], ['poolw32'], 'c0')
        dma('sp', poolv[:], poolv_d, [], ['poolv'], 'c0')
        dma('sp', gqk[:], gqk_d, [], ['gqk'], 'c0')
        cpy(ident[:], cm_ident, ['cmb'], ['ident'])
        cpy(ones[:], cm_ones, ['cmb'], ['ones'])
        cpy(bdiag[:], cm_bdiag, ['cmb'], ['bdiag'])
        cpy(cmask[:], cm_cmask.rearrange("p (h n) -> p h n", h=4), ['cmb'], ['cmask'])
        cpy(poolw[:], poolw32, ['poolw32'], ['poolw'])
        ts(gqk[:, 0:1], gqk[:, 0:1], 0.125, None, ALU.mult, None, ['gqk'], ['gqk'])

        rb = tmpf[0:16, 0, 0:512]
        rb2 = tmpf[0:16, 1, 0:512]
        dma('sp', rb[:, 0:320], relb_d, [], ['rb'], 'c2')
        cpy(rb2[:, 0:320], rb[:, 319:320].to_broadcast([16, 320]), ['rb'], ['rb2'])
        dma('sp', rext_t.ap()[:, 0:320], rb[:, 0:320], ['rb'], ['rext'], 'c3')
        dma('sp', rext_t.ap()[:, 320:640], rb2[:, 0:320], ['rb2'], ['rext'], 'c3')
        bias32 = scr[:, 8192:8192 + 2 * 7 * 1024].bitcast(F32).rearrange("p (t h q) -> p t h q", t=7, h=16)
        for p in range(128):
            kl = p % 64
            if p < 64:
                t0, d0 = 0, 0
            else:
                t0, d0 = 1, 0
            ntile = 7 - t0
            src = bass.AP(tensor=rext_t, offset=64 * d0 + 63 - kl, ap=[[0, 1], [64, ntile], [640, 16], [1, 64]])
            q = 'sp' if p % 2 == 0 else 'pool'
            dma(q, bias32[p:p + 1, t0:7, :, :], src, ['rext'], ['bias32'], 'c4')
        memset(bias32[64:128, 0, :, :], 0.0, ['bias32'])
        cpy(bias[:], bias32.rearrange("p t h q -> p t (h q)"), ['bias32'], ['bias'])

        act(cact[:], cT, AF.Silu, ['cT'], ['cact'])
        aws = scr[:, 8192:16384].rearrange("p (s f) -> p s f", s=2)
        P.seed([('aws', 0), ('aws', 1)], ['bias32'])
        gi = 0
        for l in range(4):
            for b in range(18):
                s = gi % 2
                gi += 1
                wv = aws[:, s, :].rearrange("p (k n) -> p k n", k=8)
                dma('pool', wv, adaw_d[l].rearrange("(k p) n -> p k n", p=128)[:, :, 512 * b:512 * b + 512],
                    [], [('aws', s)], ('aws', s))
                for j in range(4):
                    ncn = b * 4 + j
                    bk = bankA()
                    for kc in range(8):
                        mm(ps[:, bk, 0:NS], wv[:, kc, 128 * j:128 * j + 128], cact[:, kc, :], kc == 0, kc == 7,
                           [('aws', s), 'cact'], [('ps', bk)])
                    act(mod[:, l, ncn, :], ps[:, bk, 0:NS], AF.Identity, [('ps', bk), 'adab'], ['mod'],
                        bias=adab[:, l, ncn:ncn + 1])
        for l in range(4):
            for sub in range(3):
                scj = 1 + 3 * sub
                gj = 2 + 3 * sub
                ts(Atab[:, l, sub, :, :], mod[:, l, scj * 8:scj * 8 + 8, :], 1.0, None, ALU.add, None, ['mod'], ['Atab'])
                tt(Atab[:, l, sub, :, :], Atab[:, l, sub, :, :],
                   normg[:, l, sub, :].rearrange("p (c o) -> p c o", o=1).to_broadcast([128, NCH, NS]), ALU.mult,
                   ['Atab', 'normg'], ['Atab'])
                ts(Gtab[:, l, sub, :, :], mod[:, l, gj * 8:gj * 8 + 8, :], 1.0 if sub == 1 else 0.5, None, ALU.mult, None,
                   ['mod'], ['Gtab'])
                cpy(Btab[:, l, sub, :, :], mod[:, l, (3 * sub) * 8:(3 * sub) * 8 + 8, :], ['mod'], ['Btab'])
        tt(poolA[:, 0, :, :], Gtab[:, 1, 1, :, :], poolv[:, 1, :].rearrange("p (c o) -> p c o", o=1).to_broadcast([128, NCH, NS]),
           ALU.mult, ['Gtab', 'poolv'], ['poolA'])
        tt(poolA[:, 1, :, :], poolA[:, 0, :, :], poolv[:, 0, :].rearrange("p (c o) -> p c o", o=1).to_broadcast([128, NCH, NS]),
           ALU.mult, ['poolA', 'poolv'], ['poolA'])

        scr_keys = [[('aws', 0), ('aws', 1), 'bias32', 'mod', 'poolw32', 'cmb', 'adab', 'cT']]

        def scr_phase(new_keys):
            P.seed(new_keys, scr_keys[0])
            scr_keys[0] = list(new_keys)

        XK = [('x', c) for c in range(NCH)]
        HK = [('h', c) for c in range(NCH)]

        def norm(l, sub, segs, T, out_ap_fn, out_keys):
            bk = bankA()
            for c in range(NCH):
                act(sq[:, c % 2, 0:T], xT[:, c, 0:T], AF.Square, [('x', c)], [('sq', c % 2)])
                mm(ps[:, bk, 0:T], ones[:], sq[:, c % 2, 0:T], c == 0, c == NCH - 1, [('sq', c % 2), 'ones'], [('ps', bk)])
            act(rstd[:, 0:T], ps[:, bk, 0:T], AF.Sqrt, [('ps', bk)], ['rstd'], bias=EPS, scale=1.0 / D)
            P.add('dve', lambda e: e.reciprocal(out=rstd[:, 0:T], in_=rstd[:, 0:T]), ['rstd'], ['rstd'])
            for c in range(NCH):
                ti = tmp32()
                for (s, c0, c1) in segs:
                    stt(tmpf[:, ti, c0:c1], xT[:, c, c0:c1], Atab[:, l, sub, c, s:s + 1], rstd[:, c0:c1], ALU.mult, ALU.mult,
                        [('x', c), 'rstd', 'Atab'], [('tmp', ti)])
                    act(out_ap_fn(c, c0, c1), tmpf[:, ti, c0:c1], AF.Identity, [('tmp', ti), 'Btab'], [out_keys[c]],
                        bias=Btab[:, l, sub, c, s:s + 1])

        def residual(bk, n, segs, gate_fn, extra_reads=()):
            for (s, c0, c1) in segs:
                stt(xT[:, n, c0:c1], ps[:, bk, c0:c1], gate_fn(n, s), xT[:, n, c0:c1], ALU.mult, ALU.add,
                    [('ps', bk), ('x', n), 'Gtab'] + list(extra_reads), [('x', n)])

        def proj_fm(wname, nblocks, KC, BW, rhs_fn, rhs_keys, T, consumer):
            nper = BW // 128
            for b in range(nblocks):
                s, wv = w_next(wname)
                for j in range(nper):
                    bk = bankA()
                    for kc in range(KC):
                        mm(ps[:, bk, 0:T], wv[:, kc, 128 * j:128 * j + 128], rhs_fn(kc), kc == 0, kc == KC - 1,
                           [('ws', s)] + list(rhs_keys), [('ps', bk)])
                    consumer(b * nper + j, bk)

        def ffn(l, which, segs, T):
            sub = 0 if which == 1 else 2
            norm(l, sub, segs, T, lambda c, c0, c1: hT[:, c, c0:c1], HK)
            hid = scr[:, 0:22 * TILE].rearrange("p (c n) -> p c n", c=22)
            hk = [('hid', c) for c in range(22)]
            scr_phase(hk)
            wname = 'ffn%d_in' % which
            for b in range(11):
                s, wv = w_next(wname)
                for j in range(2):
                    ci = 2 * b + j
                    bg = bankA()
                    for kc in range(8):
                        mm(ps[:, bg, 0:T], wv[:, kc, 128 * j:128 * j + 128], hT[:, kc, 0:T], kc == 0, kc == 7,
                           [('ws', s)] + HK, [('ps', bg)])
                    bu = bankA()
                    for kc in range(8):
                        mm(ps[:, bu, 0:T], wv[:, kc, 256 + 128 * j:256 + 128 * j + 128], hT[:, kc, 0:T], kc == 0, kc == 7,
                           [('ws', s)] + HK, [('ps', bu)])
                    ti = tmp32()
                    act(tmpf[:, ti, 0:T], ps[:, bg, 0:T], AF.Silu, [('ps', bg)], [('tmp', ti)])
                    tt(hid[:, ci, 0:T], tmpf[:, ti, 0:T], ps[:, bu, 0:T], ALU.mult, [('tmp', ti), ('ps', bu)], [('hid', ci)])
            proj_fm('ffn%d_out' % which, 8, 22, 128, lambda kc: hid[:, kc, 0:T], hk, T,
                    lambda n, bk: residual(bk, n, segs, lambda n_, s_: Gtab[:, l, sub, n_, s_:s_ + 1]))

        def retention(l, j, segs, T, blocks, tinfo):
            norm(l, 1, segs, T, lambda c, c0, c1: hT[:, c, c0:c1], HK)
            nb = len(blocks)
            gz = scr[:, 0:8192].rearrange("p (b n) -> p b n", b=4)
            qk = scr[:, 8192:16384].rearrange("p (c n) -> p c n", c=16)
            vv = scr[:, 16384:24576].rearrange("p (b n) -> p b n", b=4)
            gk = [('gz', b) for b in range(4)]
            qkk = [('qk', b) for b in range(4)]
            vk = [('v', b) for b in range(4)]
            scr_phase(gk + qkk + vk)
            bl_of = blocks
            def qk_block(b, which):
                s, wv = w_next('ret_in')
                for hh in range(2):
                    h = 2 * (b % 2) + hh
                    b1 = bankA()
                    for kc in range(8):
                        mm(ps[:, b1, 0:T], wv[:, kc, 256 * hh:256 * hh + 128], hT[:, kc, 0:T], kc == 0, kc == 7,
                           [('ws', s)] + HK, [('ps', b1)])
                    b2 = bankA()
                    for kc in range(8):
                        mm(ps[:, b2, 0:T], wv[:, kc, 256 * hh + 128:256 * hh + 256], hT[:, kc, 0:T], kc == 0, kc == 7,
                           [('ws', s)] + HK, [('ps', b2)])
                    dec = qdec if which == 0 else kdec
                    t1, t2, t3, t4 = tmp32(), tmp32(), tmp32(), tmp32()
                    tt(tmpf[:, t1, 0:T], ps[:, b1, 0:T], cs[:, 0, 0:T], ALU.mult, [('ps', b1), 'cs'], [('tmp', t1)])
                    tt(tmpf[:, t2, 0:T], ps[:, b2, 0:T], cs[:, 1, 0:T], ALU.mult, [('ps', b2), 'cs'], [('tmp', t2)])
                    tt(tmpf[:, t3, 0:T], ps[:, b1, 0:T], cs[:, 1, 0:T], ALU.mult, [('ps', b1), 'cs'], [('tmp', t3)])
                    tt(tmpf[:, t4, 0:T], ps[:, b2, 0:T], cs[:, 0, 0:T], ALU.mult, [('ps', b2), 'cs'], [('tmp', t4)])
                    tt(tmpf[:, t1, 0:T], tmpf[:, t1, 0:T], tmpf[:, t2, 0:T], ALU.subtract, [('tmp', t1), ('tmp', t2)], [('tmp', t1)])
                    tt(tmpf[:, t3, 0:T], tmpf[:, t3, 0:T], tmpf[:, t4, 0:T], ALU.add, [('tmp', t3), ('tmp', t4)], [('tmp', t3)])
                    for bi, (c0, L, s_, fs, ls, si) in enumerate(bl_of):
                        for half, tsrc in ((0, t1), (1, t3)):
                            ch = 8 * which + 2 * h + half
                            tt(qk[:, ch, c0:c0 + L], tmpf[:, tsrc, c0:c0 + L], dec[:, h, 0:L], ALU.mult,
                               [('tmp', tsrc), 'cm'], [('qk', bi)])
            for b in range(2):
                qk_block(b, 0)
            for b in range(2):
                qk_block(b, 1)
            for b in range(4):
                s, wv = w_next('ret_in')
                for bi, (c0, L, s_, fs, ls, si) in enumerate(bl_of):
                    bk = bankA()
                    for kc in range(8):
                        mm(ps[0:L, bk, :], hT[:, kc, c0:c0 + L], wv[:, kc, :], kc == 0, kc == 7, [('ws', s)] + HK, [('ps', bk)])
                    cpy(vv[0:L, bi, 512 * b:512 * b + 512], ps[0:L, bk, :], [('ps', bk)], [('v', bi)])
            for b in range(4):
                s, wv = w_next('ret_in')
                for bi, (c0, L, s_, fs, ls, si) in enumerate(bl_of):
                    bk = bankA()
                    for kc in range(8):
                        mm(ps[0:L, bk, :], hT[:, kc, c0:c0 + L], wv[:, kc, :], kc == 0, kc == 7, [('ws', s)] + HK, [('ps', bk)])
                    act(gz[0:L, bi, 512 * b:512 * b + 512], ps[0:L, bk, :], AF.Silu, [('ps', bk)], [('gz', bi)])
            for bi, (c0, L, s_, fs, ls, si) in enumerate(bl_of):
                gam = [1.0 - 2.0 ** (-5.0 - h) for h in range(RET_H)]
                if fs:
                    if si['kind'] == 'prompt':
                        memset(Sst[:, j, :, :, :], 0.0, [('S', j)])
                    else:
                        for h_ in range(RET_H):
                            dma('pool', Sst[:, j, h_, :, :], sret_d[j, si['idx'], h_].rearrange("(a p) d -> p a d", p=128),
                                [], [('S', j)], 'Sld')

                if fs or bi == 0:
                    act(Sbf[:].rearrange("p h a d -> p (h a d)"), Sst[:, j, :, :, :].rearrange("p h a d -> p (h a d)"), AF.Copy,
                        [('S', j)], ['Sbf'])
                bt = bankA()
                for h in range(RET_H):
                    for half in range(2):
                        tr(psb[0:L, bt, 256 * h + 128 * half:256 * h + 128 * half + 128], qk[:, 8 + 2 * h + half, c0:c0 + L],
                           ident[:], [('qk', bi), 'ident'], [('ps', bt)])
                for h in range(RET_H):
                    act(ktok[0:L, 256 * h:256 * h + 256], psb[0:L, bt, 256 * h:256 * h + 256], AF.Copy, [('ps', bt)], ['ktok'],
                        scale=float(gam[h] ** L))
                bi_ = bankA()
                for h in range(RET_H):
                    for half in range(2):
                        mm(ps[0:L, bi_, 128 * h:128 * h + L], qk[:, 8 + 2 * h + half, c0:c0 + L], qk[:, 2 * h + half, c0:c0 + L],
                           half == 0, half == 1, [('qk', bi)], [('ps', bi_)])
                tt(innT[0:L, :, 0:L], ps[0:L, bi_, :].rearrange("p (h n) -> p h n", h=4)[:, :, 0:L], cmask[0:L, :, 0:L], ALU.mult,
                   [('ps', bi_), 'cmask'], ['innT'])
                obanks = []
                for h in range(RET_H):
                    bo = bankB()
                    mm(ps[0:L, bo, :], innT[0:L, h, 0:L], vv[0:L, bi, 512 * h:512 * h + 512], True, False,
                       ['innT', ('v', bi)], [('ps', bo)])
                    for half in range(2):
                        mm(ps[0:L, bo, :], qk[:, 2 * h + half, c0:c0 + L], Sbf[:, h, half, :], False, half == 1,
                           [('qk', bi), 'Sbf'], [('ps', bo)])
                    oi = h % 2
                    act(osb[0:L, oi, :], ps[0:L, bo, :], AF.Copy, [('ps', bo)], [('osb', oi)])
                    P.add('dve', lambda e, L=L, oi=oi, h=h: e.bn_stats(out=stat[0:L, h, 0:6], in_=osb[0:L, oi, :]),
                          [('osb', oi)], [('stat', h)])
                    P.add('dve', lambda e, L=L, h=h: e.bn_aggr(out=mv[0:L, h, :], in_=stat[0:L, h, 0:6]), [('stat', h)], [('mv', h)])
                    act(rsd[0:L, h:h + 1], mv[0:L, h, 1:2], AF.Sqrt, [('mv', h)], [('rsd', h)], bias=EPS, scale=1.0)
                    P.add('dve', lambda e, L=L, h=h: e.reciprocal(out=rsd[0:L, h:h + 1], in_=rsd[0:L, h:h + 1]), [('rsd', h)], [('rsd', h)])
                    ts(osb[0:L, oi, :], osb[0:L, oi, :], mv[0:L, h, 0:1], rsd[0:L, h:h + 1], ALU.subtract, ALU.mult,
                       [('osb', oi), ('mv', h), ('rsd', h)], [('osb', oi)])
                    tt(gz[0:L, bi, 512 * h:512 * h + 512], osb[0:L, oi, :], gz[0:L, bi, 512 * h:512 * h + 512], ALU.mult,
                       [('osb', oi), ('gz', bi)], [('gz', bi)])
                for h in range(RET_H):
                    for half in range(2):
                        bs = bankA()
                        mm(ps[:, bs, :], ktok[0:L, 256 * h + 128 * half:256 * h + 128 * half + 128], vv[0:L, bi, 512 * h:512 * h + 512],
                           True, True, ['ktok', ('v', bi)], [('ps', bs)])
                        stt(Sst[:, j, h, half, :], Sst[:, j, h, half, :], float(gam[h] ** L), ps[:, bs, :], ALU.mult, ALU.add,
                            [('S', j), ('ps', bs)], [('S', j)])
                        act(Sbf[:, h, half, :], Sst[:, j, h, half, :], AF.Copy, [('S', j)], ['Sbf'])
                if ls:
                    for h_ in range(RET_H):
                        dst = (retp_d if si['kind'] == 'prompt' else rets_d)[j, si['idx'], h_].rearrange("(a p) d -> p a d", p=128)
                        dma('pool', dst, Sst[:, j, h_, :, :], [('S', j)], [], 'Sst')
                for half8 in range(2):
                    bz = bankA()
                    for cc in range(8):
                        c = 8 * half8 + cc
                        tr(psb[:, bz, 128 * cc:128 * cc + L], gz[0:L, bi, 128 * c:128 * c + 128], ident[0:L, 0:L],
                           [('gz', bi), 'ident'], [('ps', bz)])
                    for cc in range(8):
                        c = 8 * half8 + cc
                        act(qk[:, c, c0:c0 + L], psb[:, bz, 128 * cc:128 * cc + L], AF.Copy, [('ps', bz), 'gngT'], [('qk', bi)],
                            scale=gngT[:, j, c:c + 1])
            proj_fm('ret_out', 4, 16, 256, lambda kc: qk[:, kc, 0:T], qkk, T,
                    lambda n, bk: residual(bk, n, segs, lambda n_, s_: Gtab[:, l, 1, n_, s_:s_ + 1]))

        def poolmix(l, segs, T, tinfo):
            hext = scr[:, 0:2 * NCH * 528].bitcast(F32).rearrange("p (c n) -> p c n", c=NCH)
            wsA = scr[:, 8448:8448 + 2 * NCH * 528].bitcast(F32).rearrange("p (c n) -> p c n", c=NCH)
            wsB6 = scr[:, 16896:16896 + 2 * 6 * 528].bitcast(F32).rearrange("p (c n) -> p c n", c=6)

            class _Sh:
                def __getitem__(self_, key):
                    p_, c_, n_ = key
                    return wsB6[p_, slice(c_.start - 2, c_.stop - 2), n_]
            wsB = _Sh()
            pooled = hT
            keys = ['hext', 'wsA', 'wsB']
            scr_phase(keys)
            norm(l, 1, segs, T, lambda c, c0, c1: hext[:, c, 16 + c0:16 + c1], ['hext'] * NCH)
            for (s, c0, c1) in segs:
                si = tinfo['seqinfo'][s]
                Ls = c1 - c0
                if si['first']:
                    if si['kind'] == 'prompt':
                        memset(hext[:, :, c0 + 1:c0 + 16], 0.0, ['hext'])
                    else:
                        dma('pool', hext[:, :, c0 + 1:c0 + 16], spool_d[si['idx']].rearrange("(c p) n -> p c n", p=128),
                            [], ['hext'], 'pl')
                else:
                    cpy(hext[:, :, c0 + 1:c0 + 16], poolcar[:], ['poolcar', 'hext'], ['hext'])
                memset(hext[:, :, c0:c0 + 1], 0.0, ['hext'])
                E = lambda buf, sh, c_lo, c_hi: buf[:, c_lo:c_hi, c0 + 16 - sh:c0 + 16 - sh + Ls]
                Ex = lambda buf, sh, c_lo, c_hi, lo: buf[:, c_lo:c_hi, c0 + lo - sh:c0 + 16 + Ls - sh]
                tt(Ex(wsA, 0, 0, 8, 2), Ex(hext, 0, 0, 8, 2), Ex(hext, 1, 0, 8, 2), ALU.add, ['hext'], ['wsA'])
                tt(Ex(wsB, 0, 2, 8, 4), Ex(wsA, 0, 2, 8, 4), Ex(wsA, 2, 2, 8, 4), ALU.add, ['wsA'], ['wsB'])
                tt(Ex(wsA, 0, 4, 8, 8), Ex(wsB, 0, 4, 8, 8), Ex(wsB, 4, 4, 8, 8), ALU.add, ['wsB', 'wsA'], ['wsA'])
                tt(Ex(wsB, 0, 6, 8, 16), Ex(wsA, 0, 6, 8, 16), Ex(wsA, 8, 6, 8, 16), ALU.add, ['wsA', 'wsB'], ['wsB'])
                srcs = [(wsA, 0, 2, 2.0), (wsB, 2, 4, 4.0), (wsA, 4, 6, 8.0), (wsB, 6, 8, 16.0)]
                if si['kind'] == 'prompt' and si['first']:
                    for (buf, lo, hi, w) in srcs:
                        tt(buf[:, lo:hi, c0 + 16:c0 + 32], buf[:, lo:hi, c0 + 16:c0 + 32], pratio[:, lo:hi, :], ALU.mult,
                           ['wsA', 'wsB', 'cm'], ['wsA', 'wsB'])
                for (buf, lo, hi, w) in srcs:
                    stt(pooled[:, lo:hi, c0:c1], E(buf, 0, lo, hi), 1.0 / w, E(hext, 0, lo, hi), ALU.mult, ALU.subtract,
                        ['wsA', 'wsB', 'hext'], HK)
                if si['last']:
                    dst = (poolp_d if si['kind'] == 'prompt' else pools_d)[si['idx']].rearrange("(c p) n -> p c n", p=128)
                    dma('pool', dst, hext[:, :, c0 + 1 + Ls:c0 + 16 + Ls], ['hext'], [], 'plo')
                else:
                    cpy(poolcar[:], hext[:, :, c0 + 1 + Ls:c0 + 16 + Ls], ['hext'], ['poolcar'])
            for g in range(4):
                for ec in range(2):
                    n = 2 * g + ec
                    bk = bankA()
                    for cc in range(2):
                        mm(ps[:, bk, 0:T], poolw[:, g, cc, 128 * ec:128 * ec + 128], pooled[:, 2 * g + cc, 0:T], cc == 0, cc == 1,
                           ['poolw'] + HK, [('ps', bk)])
                    for (s, c0, c1) in segs:
                        ti = tmp32()
                        act(tmpf[:, ti, c0:c1], ps[:, bk, c0:c1], AF.Identity, [('ps', bk), 'poolA'], [('tmp', ti)],
                            bias=poolA[:, 1, n, s:s + 1], scale=poolA[:, 0, n, s:s + 1])
                        tt(xT[:, n, c0:c1], xT[:, n, c0:c1], tmpf[:, ti, c0:c1], ALU.add, [('tmp', ti), ('x', n)], [('x', n)])

        def attention(l, segs, T, blocks, tinfo):
            astop = cfg.att_stop if tinfo['kind'] == 'prompt' else 99
            norm(l, 1, segs, T, lambda c, c0, c1: hT[:, c, c0:c1], HK)
            qT = scr[:, 0:4096].rearrange("p (c n) -> p c n", c=8)
            kT = scr[:, 4096:12288].rearrange("p (c n) -> p c n", c=8)
            V = scr[:, 12288:12288 + 8 * 1152].rearrange("p (b h d) -> p b h d", b=8, h=16)
            PT = scr[:, 21504:21504 + 5 * 512].rearrange("p (t n) -> p t n", t=5)
            QK = [('qT', i) for i in range(8)]
            qTo = Sbf[:].rearrange("p h a d -> p (h a) d")
            keys = QK + ['kTp', 'kTc', 'Vp', 'Vc'] + [('PT', i) for i in range(5)]
            scr_phase(keys)
            kind = tinfo['kind']
            for b_ in range(8):
                memset(V[:, b_, :, 64:72], 1.0, ['Vp', 'Vc'])
            def qk_consume(which):
                def f(n, bk):
                    nn = n % 8
                    ti = tmp32()
                    cpy(tmpf[:, ti, 0:T], ps[:, bk, 0:T], [('ps', bk)], [('tmp', ti)])
                    act(sq[:, nn % 2, 0:T], tmpf[:, ti, 0:T], AF.Square, [('tmp', ti)], [('sq', nn % 2)])
                    b2 = bankA()
                    mm(ps[:, b2, 0:T], bdiag[:], sq[:, nn % 2, 0:T], True, True, [('sq', nn % 2), 'bdiag'], [('ps', b2)])
                    t2 = tmp32()
                    act(tmpf[:, t2, 0:T], ps[:, b2, 0:T], AF.Sqrt, [('ps', b2)], [('tmp', t2)], bias=EPS, scale=1.0 / 64)
                    P.add('dve', lambda e, t2=t2: e.reciprocal(out=tmpf[:, t2, 0:T], in_=tmpf[:, t2, 0:T]), [('tmp', t2)], [('tmp', t2)])
                    if which == 0:
                        stt(qT[:, nn, 0:T], tmpf[:, ti, 0:T], gqk[:, 0:1], tmpf[:, t2, 0:T], ALU.mult, ALU.mult,
                            [('tmp', ti), ('tmp', t2), 'gqk'], QK)
                        cpy(qTo[64:128, nn, 0:T], qT[64:128, nn, 0:T], QK, ['Sbf'])
                        memset(qTo[0:64, nn, 0:T], 0.0, ['Sbf'])
                        memset(qT[64:128, nn, 0:T], 0.0, QK)
                    else:
                        stt(tmpf[:, ti, 0:T], tmpf[:, ti, 0:T], gqk[:, 1:2], tmpf[:, t2, 0:T], ALU.mult, ALU.mult,
                            [('tmp', ti), ('tmp', t2), 'gqk'], [('tmp', ti)])
                        cpy(kT[:, nn, 512:512 + T], tmpf[:, ti, 0:T], [('tmp', ti)], ['kTc'], eng='act')
                return f
            if astop <= 0:
                return
            proj_fm('att_qkv', 2, 8, 512, lambda kc: hT[:, kc, 0:T], HK, T, qk_consume(0))
            if astop <= 0.5:
                return
            proj_fm('att_qkv', 2, 8, 512, lambda kc: hT[:, kc, 0:T], HK, T, qk_consume(1))
            if astop <= 1:
                return
            for b in range(2):
                s, wv = w_next('att_qkv')
                for bi, (c0, L, s_, fs, ls, si) in enumerate(blocks):
                    bk = bankA()
                    for kc in range(8):
                        mm(ps[0:L, bk, :], hT[:, kc, c0:c0 + L], wv[:, kc, :], kc == 0, kc == 7, [('ws', s)] + HK, [('ps', bk)])
                    cpy(V[0:L, 4 + bi, 8 * b:8 * b + 8, 0:64], ps[0:L, bk, :].rearrange("p (h d) -> p h d", h=8), [('ps', bk)], ['Vc'])
            for bi, (c0, L, s_, fs, ls, si) in enumerate(blocks):
                if si['kind'] == 'sample':
                    dma('pool', ks_d[si['idx']].rearrange("(c p) n -> p c n", p=128), kT[:, :, 512 + c0:512 + c0 + L], ['kTc'], [], 'plo')
                    dma('pool', vs_d[si['idx']].rearrange("t (h d) -> t h d", h=16), V[0:L, 4 + bi, :, 0:64], ['Vc'], [], 'plo')
                elif tinfo['tile'] == NT - 1:
                    if bi == 0:
                        dma('pool', kp_d[si['idx']].rearrange("(c p) n -> p c n", p=128), kT[:, :, 512:1024], ['kTc'], [], 'plo')
                    dma('pool', vp_d[si['idx'], c0:c0 + L, :].rearrange("t (h d) -> t h d", h=16), V[0:L, 4 + bi, :, 0:64], ['Vc'], [], 'plo')
            if astop <= 2:
                return
            def attend(qc0, own_col, own_blk, tiles):
                obk = [bankB(), bankB(), bankB()]
                nt = len(tiles)
                for hg in range(2):
                    for ti_, (kcol, nk, vblk, bt, mlow) in enumerate(tiles):
                        bk = bankA()
                        for hh in range(8):
                            h = 8 * hg + hh
                            pp = 64 * (h % 2)
                            qsrc = qT if h % 2 == 0 else qTo
                            mm(ps[0:nk, bk, 64 * hh:64 * hh + 64], kT[:, h // 2, kcol:kcol + nk], qsrc[:, h // 2, qc0:qc0 + 64],
                               True, True, ['kTp', 'kTc', ('qT', qc0 // 64), 'Sbf'], [('ps', bk)])
                        t3 = tmp32()
                        tt(tmpf[0:nk, t3, :], ps[0:nk, bk, :], bias[0:nk, bt, 512 * hg:512 * hg + 512], ALU.add,
                           [('ps', bk), 'bias'], [('tmp', t3)])
                        pi = ti_
                        act(PT[0:nk, pi, :], tmpf[0:nk, t3, :], AF.Exp, [('tmp', t3)], [('PT', pi)])
                        if mlow:
                            memset(PT[0:64, pi, :], 0.0, [('PT', pi)])
                    if astop <= 4:
                        continue
                    for hh in range(8):
                        h = 8 * hg + hh
                        ob = obk[h // 7]
                        oc = 72 * (h % 7)
                        for ti_, (kcol, nk, vblk, bt, mlow) in enumerate(tiles):
                            pi = ti_
                            mm(ps[0:64, ob, oc:oc + 72], PT[0:nk, pi, 64 * hh:64 * hh + 64], V[0:nk, vblk, h, :], ti_ == 0, ti_ == nt - 1,
                               [('PT', pi), 'Vp', 'Vc'], [('ps', ob)])
                if astop <= 5:
                    return
                osv = osb[0:64, :, :].rearrange("p a n -> p (a n)")
                for g3 in range(3):
                    nh = 7 if g3 < 2 else 2
                    pv = ps[0:64, obk[g3], 0:72 * nh].rearrange("p (h d) -> p h d", h=nh)
                    P.add('dve', lambda e, pv=pv, g3=g3, nh=nh: e.reciprocal(out=rec[0:64, 7 * g3:7 * g3 + nh], in_=pv[:, :, 64]),
                          [('ps', obk[g3])], [('rec', g3)])
                    tt(osv[:, 448 * g3:448 * g3 + 64 * nh].rearrange("p (h d) -> p h d", h=nh), pv[:, :, 0:64],
                       rec[0:64, 7 * g3:7 * g3 + nh].rearrange("p (h o) -> p h o", o=1).to_broadcast([64, nh, 64]), ALU.mult,
                       [('ps', obk[g3]), ('rec', g3)], [('osb', 0), ('osb', 1)])
                if astop <= 6:
                    return
                ob16 = ktok[0:64, :]
                cpy(ob16, osv, [('osb', 0), ('osb', 1)], ['ktok'], eng='act')
                bz = bankA()
                for c in range(8):
                    tr(psb[:, bz, 64 * c:64 * c + 64], ob16[:, 128 * c:128 * c + 128], ident[0:64, 0:64], ['ktok', 'ident'], [('ps', bz)])
                cpy(qT[:, :, qc0:qc0 + 64], psb[:, bz, 0:512].rearrange("p (c n) -> p c n", c=8), [('ps', bz)], [('qT', qc0 // 64)])

            CONSTT = 6
            if kind == 'prompt':
                tile_i = tinfo['tile']
                if tile_i > 0:
                    dma('pool', kT[:, :, 0:512], carK_d, ['carK'], ['kTp'], ('aws', 0))
                    dma('pool', V[:, 0:4, :, :].rearrange("p b h d -> p b (h d)"), carV_d, ['carV'], ['Vp'], ('aws', 0))
                else:
                    memset(kT[:, :, 448:512], 0.0, ['kTp'])
                    memset(V[:, 3, :, 0:64], 0.0, ['Vp'])
                for qc in range(T // 64):
                    n = tile_i * 8 + qc
                    own = 512 + 64 * qc
                    tiles = []
                    if qc % 2 == 0:
                        for i, bt in enumerate([CONSTT, CONSTT, 4, 2]):
                            m0 = n - 8 + 2 * i
                            if m0 >= 0:
                                tiles.append((own - 512 + 128 * i, 128, qc // 2 + i, bt, False))
                        tiles.append((own, 64, 4 + qc // 2, 0, False))
                    else:
                        for i, bt in enumerate([CONSTT, CONSTT, 5, 3, 1]):
                            m0 = n - 9 + 2 * i
                            if m0 + 1 >= 0:
                                col = own - 576 + 128 * i
                                tiles.append((col, 128, col // 128, bt, i == 0))
                    attend(64 * qc, own, None, tiles)
                if tile_i < NT - 1:
                    dma('pool', carK_d, kT[:, :, 512:1024], ['kTc'], ['carK'], ('aws', 1))
                    dma('pool', carV_d, V[:, 4:8, :, :].rearrange("p b h d -> p b (h d)"), ['Vc'], ['carV'], ('aws', 1))
            else:
                for bi, (c0, L, s_, fs, ls, si) in enumerate(blocks):
                    dma('pool', kT[:, :, 0:512], ckT_d[si['idx']].rearrange("(c p) n -> p c n", p=128), [], ['kTp'], ('aws', 0))
                    for b_ in range(4):
                        dma('pool', V[:, b_, :, 0:64], cv_d[si['idx'], 128 * b_:128 * b_ + 128, :].rearrange("p (h d) -> p h d", h=16),
                            [], ['Vp'], ('aws', 0))
                    tiles = [(128 * i, 128, i, bt, False) for i, bt in enumerate([CONSTT, CONSTT, 4, 2])]
                    tiles.append((512 + c0, 64, 4 + bi, 0, False))
                    if astop > 3:
                        attend(c0, 512 + c0, None, tiles)
            if astop < 99:
                return
            proj_fm('att_o', 2, 8, 512, lambda kc: qT[:, kc, 0:T], QK, T,
                    lambda n, bk: residual(bk, n, segs, lambda n_, s_: Gtab[:, l, 1, n_, s_:s_ + 1]))

        tiles = []
        if cfg.sample:
            seqinfo = {NSEQ + i: dict(kind='sample', idx=i, first=True, last=True) for i in range(2)}
            tiles.append(dict(kind='sample', tile=0, T=128, segs=[(NSEQ, 0, 64), (NSEQ + 1, 64, 128)], seqinfo=seqinfo,
                              blocks=[(0, 64, NSEQ, True, True, seqinfo[NSEQ]), (64, 64, NSEQ + 1, True, True, seqinfo[NSEQ + 1])],
                              pos0=SEQ))
        for sq_ in range(NSEQ):
            for t in range(NT):
                si = dict(kind='prompt', idx=sq_, first=(t == 0), last=(t == NT - 1))
                tiles.append(dict(kind='prompt', tile=t, T=TILE, segs=[(sq_, 0, TILE)], seqinfo={sq_: si},
                                  blocks=[(128 * b, 128, sq_, t == 0 and b == 0, t == NT - 1 and b == 3, si) for b in range(4)],
                                  pos0=t * TILE))
        wstate['total'] = NBLK * len(tiles)
        stage = [0]

        for tinfo in tiles:
            T = tinfo['T']
            segs = tinfo['segs']
            if tinfo['kind'] == 'prompt':
                s0 = segs[0][0]
                src = xT_d[s0].rearrange("(c p) n -> p c n", p=128)[:, :, tinfo['pos0']:tinfo['pos0'] + T]
                dma('pool', xT[:, :, 0:T], src, [], XK, 'xin')
                dma('pool', cs[:, :, 0:T], cf_d[:, :, tinfo['pos0']:tinfo['pos0'] + T], [], ['cs'], 'xin')
            else:
                for (s, c0, c1) in segs:
                    dma('pool', xT[:, :, c0:c1], xsT_d[s - NSEQ].rearrange("(c p) n -> p c n", p=128), [], XK, 'xin')
                    dma('pool', cs[:, :, c0:c1], cf_d[:, :, SEQ:SEQ + 64], [], ['cs'], 'xin')
            for l in range(4):
                for sub_ in range(3):
                    if stage[0] >= cfg.limit:
                        continue
                    stage[0] += 1
                    if sub_ == 0:
                        ffn(l, 1, segs, T)
                    elif sub_ == 2:
                        ffn(l, 2, segs, T)
                    elif l % 3 == 0:
                        retention(l, l // 3, segs, T, tinfo['blocks'], tinfo)
                    elif l % 3 == 1:
                        poolmix(l, segs, T, tinfo)
                    else:
                        attention(l, segs, T, tinfo['blocks'], tinfo)
            if tinfo['kind'] == 'prompt':
                s0 = segs[0][0]
                dst = yT_d[s0].rearrange("(c p) n -> p c n", p=128)[:, :, tinfo['pos0']:tinfo['pos0'] + T]
                dma('pool', dst, xT[:, :, 0:T], XK, [], 'xin')
            else:
                for (s, c0, c1) in segs:
                    dma('pool', ysT_d[s - NSEQ].rearrange("(c p) n -> p c n", p=128), xT[:, :, c0:c1], XK, [], 'xin')

        P.emit(nc, st)
    return nc


def host_consts(seq):
    half = 128
    inv = (10000.0 ** (-np.arange(half, dtype=np.float32) / np.float32(half))).astype(np.float32)
    pos = np.concatenate([np.arange(seq), 2048 + np.arange(64)]).astype(np.float32)
    ang = (pos[None, :] * inv[:, None]).astype(np.float32)
    cf = np.stack([np.cos(ang), np.sin(ang)], axis=1).astype(np.float32)
    gam = 1.0 - 2.0 ** (-5.0 - np.arange(4, dtype=np.float64))
    i = np.arange(128, dtype=np.float64)
    qdec = (gam[:, None] ** (i[None, :] + 1.0))
    kdec = (gam[:, None] ** (-(i[None, :] + 1.0))) * (256.0 ** -0.5)
    ident = np.eye(128)
    ones = np.ones((128, 128))
    bd = np.zeros((128, 128)); bd[0:64, 0:64] = 1; bd[64:, 64:] = 1
    cmask = (np.arange(128)[None, :] >= np.arange(128)[:, None]).astype(np.float64)
    cmask4 = np.tile(cmask[:, None, :], (1, 4, 1)).reshape(128, 512)
    ratio = np.zeros((8, 16))
    for g, w in enumerate((2, 4, 8, 16)):
        t = np.arange(16)
        ratio[2 * g:2 * g + 2, :] = (w / np.minimum(t + 1, w))[None, :]
    cm = np.concatenate([
        np.tile(qdec.reshape(1, 512), (128, 1)), np.tile(kdec.reshape(1, 512), (128, 1)),
        np.tile(ratio.reshape(1, 128), (128, 1))], axis=1).astype(np.float32)
    cmb = np.concatenate([ident, ones, bd, cmask4], axis=1).astype(np.float32)
    return cf, cm, cmb


def prep_shared(inputs, plan, offs, wtot, wrows, CW=8192):
    f = lambda a: np.asarray(a, dtype=np.float32)
    inp = {k: f(v) for k, v in inputs.items()}
    wall = np.zeros(wrows * CW, dtype=np.float32)
    for (name, l, b, kc, bw), o in zip(plan, offs):
        blk = host_block(inp, name, l, b, kc, bw)
        wall[o:o + blk.size] = blk
    sh = {}
    wl = wall.reshape(wrows, CW)
    for i in range(4):
        sh['wall%d' % i] = wl[i * (wrows // 4):(i + 1) * (wrows // 4)]
    sh['ada_w'] = np.ascontiguousarray(inp['ada_w'])
    sh['adabT'] = np.ascontiguousarray(inp['ada_b'].reshape(4, 72, 128).transpose(2, 0, 1))
    sh['normgT'] = np.ascontiguousarray(inp['norm_g'].reshape(4, 3, 8, 128).transpose(3, 0, 1, 2))
    sh['poolw'] = np.ascontiguousarray(inp['pool_w'][0].reshape(4, 2, 128, 256).transpose(2, 0, 1, 3))
    sh['poolv'] = np.ascontiguousarray(np.stack([inp['pool_b'][0].reshape(8, 128).T, inp['pool_scale'][0].reshape(8, 128).T], axis=1))
    sh['gqk'] = np.ascontiguousarray(np.stack([np.tile(inp['att_q_g'][0], 2), np.tile(inp['att_k_g'][0], 2)], axis=1))
    sh['gngT'] = np.ascontiguousarray(inp['ret_gn_g'].reshape(2, 16, 128).transpose(2, 0, 1))
    sh['relb'] = np.ascontiguousarray(inp['att_rel_bias'][0])
    return inp, sh


_CACHE = {}


def make_inmaps(inputs, cfg, ncores):
    plan, offs, wtot = weight_plan()
    CW = 8192
    wrows = (wtot + CW - 1) // CW
    wrows = ((wrows + 1023) // 1024) * 1024
    inp, sh = prep_shared(inputs, plan, offs, wtot, wrows)
    cf, cm, cmb = host_consts(cfg.seq)
    sh['cf'] = cf
    sh['cm'] = cm
    sh['cmb'] = cmb
    nq = cfg.nseq
    in_maps = []
    for c in range(ncores):
        m = dict(sh)
        ps_ = slice(nq * c, nq * c + nq)
        ss_ = slice(2 * c, 2 * c + 2)
        m['xT'] = np.ascontiguousarray(inp['x_prompt'][ps_, :cfg.seq].transpose(0, 2, 1))
        m['xsT'] = np.ascontiguousarray(inp['x_sample'][ss_].transpose(0, 2, 1))
        cc = np.concatenate([inp['c_prompt'][ps_], inp['c_sample'][ss_]], axis=0)
        m['cT'] = np.ascontiguousarray(cc.reshape(nq + 2, 8, 128).transpose(2, 1, 0))
        m['sret'] = np.ascontiguousarray(inp['state_ret'][:, ss_])
        m['spoolT'] = np.ascontiguousarray(inp['state_pool'][0, ss_].transpose(0, 2, 1))
        m['ckT'] = np.ascontiguousarray(inp['cache_att_k'][0, ss_].reshape(2, 512, 1024).transpose(0, 2, 1))
        m['cv'] = np.ascontiguousarray(inp['cache_att_v'][0, ss_].reshape(2, 512, 1024))
        in_maps.append(m)
    return in_maps


def gather(R, nprompt, nsample):
    cat = lambda k, ax: np.concatenate([r[k] for r in R], axis=ax)
    y_prompt = cat('yT', 0).transpose(0, 2, 1)
    y_sample = cat('ysT', 0).transpose(0, 2, 1)
    ret_p = cat('retp', 1)
    ret_s = cat('rets', 1)
    pool_p = cat('poolpT', 0).transpose(0, 2, 1)[None]
    pool_s = cat('poolsT', 0).transpose(0, 2, 1)[None]
    k_p = cat('kpT', 0).transpose(0, 2, 1).reshape(nprompt, 512, 16, 64)[None]
    v_p = cat('vp', 0).reshape(nprompt, 512, 16, 64)[None]
    k_s = cat('ksT', 0).transpose(0, 2, 1).reshape(nsample, 64, 16, 64)[None]
    v_s = cat('vs', 0).reshape(nsample, 64, 16, 64)[None]
    outs = (y_prompt, y_sample, ret_p, ret_s, pool_p, pool_s, k_p, v_p, k_s, v_s)
    return tuple(np.ascontiguousarray(o, dtype=np.float32) for o in outs)


def kernel(**inputs):
    try:
        import jax
        jax.config.update('jax_default_device', jax.devices('cpu')[0])
    except Exception:
        pass
    ncores = 8
    cfg = Cfg(nseq=4, seq=2048, sample=True)
    in_maps = make_inmaps(inputs, cfg, ncores)
    if 'nc' not in _CACHE:
        _CACHE['nc'] = build(cfg)
    res = run_bass_kernel_spmd(_CACHE['nc'], in_maps, core_ids=list(range(ncores)))
    return gather(res.results, 32, 16)
```
